# Optimizing a Trainium2 kernel written in Bass

```python
import math
import jax, jax.numpy as jnp
from jax import lax
import numpy as np

D_MODEL = 2048
BATCH = 4
SEQ = 4096
DEPTH = 4

CHUNK = 64
N_MIXERS = 2
BRANCH = D_MODEL
CONV_K = 3
SB_HEADS = 16
SB_HEAD_DIM = BRANCH // SB_HEADS
Q_BLOCK = 128
RMS_EPS = 1e-6

kernel_name = "hybrid_shortconv_stickbreaking_trunk"


def rmsnorm(x, g):
    xf = x.astype(jnp.float32)
    y = xf * lax.rsqrt(jnp.mean(xf * xf, axis=-1, keepdims=True) + RMS_EPS)
    return (y * g.astype(jnp.float32)).astype(x.dtype)


def causal_depthwise_conv(u, w):
    rhs = w[:, None, :]
    return lax.conv_general_dilated(
        u, rhs.astype(u.dtype), window_strides=(1,), padding=[(CONV_K - 1, 0)],
        dimension_numbers=("NWC", "WIO", "NWC"), feature_group_count=u.shape[-1])


def short_conv_mixer(h, w_in, conv_w, w_out):
    proj = jnp.einsum("bsd,de->bse", h, w_in)
    b_gate, c_gate, xt, z = jnp.split(proj, 4, axis=-1)
    y = b_gate * causal_depthwise_conv(c_gate * xt, conv_w)
    return jnp.einsum("bse,ed->bsd", jax.nn.silu(z) * y, w_out)


def stick_breaking_mixer(h, w_in, w_out):
    bsz, seq, _ = h.shape
    proj = jnp.einsum("bsd,de->bse", h, w_in)
    q, k, v, z = jnp.split(proj, 4, axis=-1)
    to_heads = lambda t: t.reshape(bsz, seq, SB_HEADS, SB_HEAD_DIM).transpose(0, 2, 1, 3)
    q, k, v = to_heads(q), to_heads(k), to_heads(v)
    scale = 1.0 / math.sqrt(SB_HEAD_DIM)
    outs = []
    for blk in range(seq // Q_BLOCK):
        s0 = blk * Q_BLOCK
        end = s0 + Q_BLOCK
        qb = q[:, :, s0:end]
        kb = k[:, :, :end]
        vb = v[:, :, :end]
        logits = jnp.einsum("bhqd,bhkd->bhqk", qb, kb).astype(jnp.float32) * scale
        t_idx = s0 + jnp.arange(Q_BLOCK)[:, None]
        s_idx = jnp.arange(end)[None, :]
        mask = s_idx < t_idx
        log_keep = jnp.where(mask, jax.nn.log_sigmoid(-logits), 0.0)
        tail = lax.cumsum(log_keep, axis=3, reverse=True) - log_keep
        weights = jnp.where(mask, jnp.exp(jax.nn.log_sigmoid(logits) + tail), 0.0)
        outs.append(jnp.einsum("bhqk,bhkd->bhqd", weights.astype(vb.dtype), vb))
    o = jnp.concatenate(outs, axis=2)
    o = o.transpose(0, 2, 1, 3).reshape(bsz, seq, BRANCH)
    return jnp.einsum("bse,ed->bsd", jax.nn.silu(z) * o, w_out)


def setup_inputs(seed: int = 0) -> dict:
    key = jax.random.key(seed)
    keys = iter(jax.random.split(key, 64))
    nrm = lambda shape, s: jax.random.normal(next(keys), shape, jnp.float32) * s
    gain = lambda: 1.0 + nrm((D_MODEL,), 0.02)
    inp = {"x": nrm((BATCH, SEQ, D_MODEL), 1.0)}
    for i in range(DEPTH):
        inp[f"ln_pre_{i}"] = gain()
        if i % N_MIXERS == 0:
            inp[f"conv_w_in_{i}"] = nrm((D_MODEL, 4 * BRANCH), D_MODEL ** -0.5)
            inp[f"conv_w_{i}"] = nrm((CONV_K, BRANCH), CONV_K ** -0.5)
            inp[f"conv_w_out_{i}"] = nrm((BRANCH, D_MODEL), BRANCH ** -0.5)
        else:
            inp[f"sb_w_in_{i}"] = nrm((D_MODEL, 4 * BRANCH), D_MODEL ** -0.5)
            inp[f"sb_w_out_{i}"] = nrm((BRANCH, D_MODEL), BRANCH ** -0.5)
        inp[f"ln_post_{i}"] = gain()
    return inp


def reference(x,
              ln_pre_0, conv_w_in_0, conv_w_0, conv_w_out_0, ln_post_0,
              ln_pre_1, sb_w_in_1, sb_w_out_1, ln_post_1,
              ln_pre_2, conv_w_in_2, conv_w_2, conv_w_out_2, ln_post_2,
              ln_pre_3, sb_w_in_3, sb_w_out_3, ln_post_3):
    layers = [
        (ln_pre_0, (conv_w_in_0, conv_w_0, conv_w_out_0), ln_post_0),
        (ln_pre_1, (sb_w_in_1, sb_w_out_1), ln_post_1),
        (ln_pre_2, (conv_w_in_2, conv_w_2, conv_w_out_2), ln_post_2),
        (ln_pre_3, (sb_w_in_3, sb_w_out_3), ln_post_3),
    ]
    h = x
    for i in range(DEPTH):
        g_pre, params, g_post = layers[i]
        u = rmsnorm(h, g_pre)
        if i % N_MIXERS == 0:
            m = short_conv_mixer(u, *params)
        else:
            m = stick_breaking_mixer(u, *params)
        h = h + rmsnorm(m, g_post)
    return h
```

```python
import math
import numpy as np
import concourse.bass as bass
import concourse.mybir as mybir
from concourse.bass_utils import run_bass_kernel_spmd

F32 = mybir.dt.float32
BF16 = mybir.dt.bfloat16
AF = mybir.ActivationFunctionType
ALU = mybir.AluOpType

D = 2048
B = 4
S = 4096
DEPTH = 4
H = 16
DH = 128
T = 512
NT = S // T
KC = D // 128
EPS = 1e-6
NCORES = 8


class Sem:
    def __init__(self, nc, stack, name):
        self.h = stack.enter_context(nc.semaphore(name))
        self.name = name
        self.issued = 0


class Buf:
    def __init__(self, name):
        self.name = name
        self.w = {}
        self.r = {}


class Eng:
    def __init__(self, nc, stack, name, handle, is_queue=False):
        self.name = name
        self.h = handle
        self.sem = None if is_queue else Sem(nc, stack, "s_" + name)
        self.seen = {}

    def wait(self, sem, val):
        if val <= 0:
            return
        if self.seen.get(sem, 0) >= val:
            return
        self.h.wait_ge(sem.h, val)
        self.seen[sem] = val


class Ctx:
    def __init__(self, nc, stack):
        self.nc = nc
        self.stack = stack
        self.pe = Eng(nc, stack, "pe", nc.tensor)
        self.act = Eng(nc, stack, "act", nc.scalar)
        self.dve = Eng(nc, stack, "dve", nc.vector)
        self.sync = Eng(nc, stack, "sync", nc.sync, is_queue=True)
        self.pool = Eng(nc, stack, "pool", nc.gpsimd, is_queue=True)
        self.engs = [self.pe, self.act, self.dve, self.sync, self.pool]
        self.sems = [self.pe.sem, self.act.sem, self.dve.sem]
        self.nbuf = 0

    def buf(self, name):
        return Buf(name)

    def dsem(self, name):
        s = Sem(self.nc, self.stack, "d_" + name)
        self.sems.append(s)
        return s

    def _waits(self, eng, reads, writes):
        need = {}
        for b in reads:
            for s, v in b.w.items():
                need[s] = max(need.get(s, 0), v)
        for b in writes:
            for s, v in b.w.items():
                if s is eng.sem:
                    continue
                need[s] = max(need.get(s, 0), v)
            for s, v in b.r.items():
                if s is eng.sem:
                    continue
                need[s] = max(need.get(s, 0), v)
        for s, v in need.items():
            eng.wait(s, v)

    def op(self, eng, fn, reads=(), writes=(), inc=True):
        self._waits(eng, reads, writes)
        ins = fn()
        if inc:
            ins.then_inc(eng.sem.h, 1)
            eng.sem.issued += 1
            v = eng.sem.issued
            for b in writes:
                b.w[eng.sem] = v
            for b in reads:
                b.r[eng.sem] = v
        return ins

    def dma(self, q, sem, out, in_, reads=(), writes=(), **kw):
        self._waits(q, reads, writes)
        q.h.dma_start(out=out, in_=in_, **kw).then_inc(sem.h, 16)
        sem.issued += 16
        for b in writes:
            b.w[sem] = sem.issued
        for b in reads:
            b.r[sem] = sem.issued

    def barrier(self):
        for e in self.engs:
            for s in self.sems:
                e.wait(s, s.issued)


def build_program(depth=DEPTH, ntiles=NT):
    from contextlib import ExitStack
    nc = bass.Bass("TRN2", target_bir_lowering=False)
    dt = nc.dram_tensor
    xT = dt("xT", [NT, 128, KC, T], F32, kind="ExternalInput").ap()
    yT = dt("yT", [NT, 128, KC, T], F32, kind="ExternalOutput").ap()
    w_in, w_out, g_pre, g_post, cw = [], [], [], [], []
    for l in range(depth):
        w_in.append(dt(f"w_in{l}", [16, 128, 8, 1024], F32, kind="ExternalInput").ap())
        w_out.append(dt(f"w_out{l}", [4, 128, 8, 1024], F32, kind="ExternalInput").ap())
        g_pre.append(dt(f"g_pre{l}", [128, KC], F32, kind="ExternalInput").ap())
        g_post.append(dt(f"g_post{l}", [128, KC], F32, kind="ExternalInput").ap())
        if l % 2 == 0:
            cw.append(dt(f"cw{l}", [128, KC * 3], F32, kind="ExternalInput").ap())
        else:
            cw.append(None)
    cmat = dt("cmat", [128, 3 * 128 + 4 * T], F32, kind="ExternalInput").ap()
    hbuf = dt("hbuf", [NT, 128, KC, T], F32, kind="Internal").ap()
    QT = dt("QT", [H, 128, S], BF16, kind="Internal").ap()
    KT = dt("KT", [H, 128, S], BF16, kind="Internal").ap()
    ZT = dt("ZT", [H, 128, S], BF16, kind="Internal").ap()
    VV = dt("VV", [H, 128, S // 128, DH], BF16, kind="Internal").ap()

    with ExitStack() as st:
        cx_ = Ctx(nc, st)
        pe, act, dve, sync, pool = cx_.pe, cx_.act, cx_.dve, cx_.sync, cx_.pool
        sb = lambda name, shape, dtype: st.enter_context(nc.sbuf_tensor(name, shape, dtype))

        hst = sb("hst", [128, KC, T], F32)
        gated = sb("gated", [128, 2, KC, T], BF16)
        wbuf = sb("wbuf", [128, 2, 8, 1024], BF16)
        mbuf = sb("mbuf", [128, KC, T], F32)
        cst = sb("cst", [128, 3 * 128 + 4 * T], BF16)
        gpre_sb = sb("gpre_sb", [128, depth, KC], F32)
        gpost_sb = sb("gpost_sb", [128, depth, KC], F32)
        cw_sb = sb("cw_sb", [128, depth, KC * 3], F32)
        rstd = sb("rstd", [128, 2, T], F32)
        lnt = sb("lnt", [128, T], F32)
        sqr = sb("sqr", [128, 2, T], BF16)
        tmpf = sb("tmpf", [128, 2, T], F32)
        ps = st.enter_context(nc.psum_tensor("ps", [128, 8, T], F32))

        B_hst = Buf("hst"); D_hst = cx_.dsem("hst")
        B_gated = [Buf("gated0"), Buf("gated1")]
        B_w = [Buf("w0"), Buf("w1")]; D_w = [cx_.dsem("w0"), cx_.dsem("w1")]
        B_m = Buf("m")
        B_cst = Buf("cst"); D_cst = cx_.dsem("cst")
        B_rstd = [Buf("rstd0"), Buf("rstd1")]
        B_lnt = Buf("lnt")
        B_sqr = [Buf("sqr0"), Buf("sqr1")]
        B_tmpf = [Buf("tmpf0"), Buf("tmpf1")]
        B_ps = [Buf(f"ps{i}") for i in range(8)]
        B_dram_h = [Buf(f"h{t}") for t in range(NT)]
        state = {"bank": 0, "w": 0, "sq": 0, "tf": 0, "ring": 8}

        def next_bank():
            b = state["bank"] % state["ring"]; state["bank"] = (b + 1) % state["ring"]
            return b

        ones = cst[:, 0:128]
        negtri = cst[:, 128:256]
        negones = cst[:, 256:384]
        masks = [cst[:, 384 + d * T: 384 + (d + 1) * T] for d in range(4)]

        CW = 3 * 128 + 4 * T
        for c0 in range(0, CW, 1216):
            cx_.dma(pool, D_cst, cst[:, c0:c0 + 1216], cmat[:, c0:c0 + 1216], writes=[B_cst])
        for l in range(depth):
            cx_.dma(sync, D_cst, gpre_sb[:, l, :], g_pre[l][:], writes=[B_cst])
            cx_.dma(sync, D_cst, gpost_sb[:, l, :], g_post[l][:], writes=[B_cst])
            if cw[l] is not None:
                cx_.dma(sync, D_cst, cw_sb[:, l, :], cw[l][:], writes=[B_cst])

        def load_w(src_tile):
            s = state["w"]; state["w"] = 1 - s
            cx_.dma(pool, D_w[s], wbuf[:, s], src_tile, writes=[B_w[s]])
            return s

        def mm_group(bank, pairs, reads, final_inc=True, out_ap=None):
            o = ps[:, bank, :] if out_ap is None else out_ap
            n = len(pairs)
            for k, (l_, r_) in enumerate(pairs):
                last = (k == n - 1)
                cx_.op(pe, lambda l_=l_, r_=r_, k=k, last=last: nc.tensor.matmul(
                    o, lhsT=l_, rhs=r_, start=(k == 0), stop=last),
                    reads=reads, writes=[B_ps[bank]], inc=(last and final_inc))

        def rstd_from_bank(bank, slot):
            cx_.op(act, lambda: nc.scalar.activation(out=lnt[:], in_=ps[:, bank, :], func=AF.Ln,
                                                     scale=1.0 / D, bias=EPS),
                   reads=[B_ps[bank]], writes=[B_lnt])
            cx_.op(act, lambda: nc.scalar.activation(out=rstd[:, slot, :], in_=lnt[:], func=AF.Exp,
                                                     scale=-0.5),
                   reads=[B_lnt], writes=[B_rstd[slot]])

        def build_uT(l, ti, src, B_src_tile, uT_ap, B_uT):
            cx_.dma(sync, D_hst, hst[:], src[ti], reads=[B_src_tile], writes=[B_hst])
            cx_.op(act, lambda: nc.scalar.activation(out=uT_ap, in_=hst[:], func=AF.Square),
                   reads=[B_hst], writes=[B_uT])
            bank = next_bank()
            mm_group(bank, [(ones, uT_ap[:, kc, :]) for kc in range(KC)], reads=[B_cst, B_uT])
            rstd_from_bank(bank, 0)
            for kc in range(KC):
                cx_.op(dve, lambda kc=kc: nc.vector.scalar_tensor_tensor(
                    out=uT_ap[:, kc, :], in0=hst[:, kc, :], scalar=gpre_sb[:, l, kc:kc + 1],
                    in1=rstd[:, 0, :], op0=ALU.mult, op1=ALU.mult),
                    reads=[B_hst, B_rstd[0], B_cst], writes=[B_uT])

        def out_proj_postnorm(l, ti, gslot, src, B_src_tile, dst, B_dst_tile):
            stat_bank = next_bank()
            for ig in range(4):
                ws = load_w(w_out[l][ig])
                for ii in range(4):
                    i = ig * 4 + ii
                    bank = next_bank()
                    if bank == stat_bank:
                        bank = next_bank()
                    mm_group(bank, [(wbuf[:, ws].rearrange("p a b -> p (a b)")[:, j * T + ii * 128: j * T + (ii + 1) * 128],
                                     gated[:, gslot, j, :]) for j in range(KC)],
                             reads=[B_w[ws], B_gated[gslot]])
                    cx_.op(act, lambda i=i, bank=bank: nc.scalar.activation(
                        out=mbuf[:, i, :], in_=ps[:, bank, :], func=AF.Copy),
                        reads=[B_ps[bank]], writes=[B_m])
                    q = state["sq"]; state["sq"] = 1 - q
                    cx_.op(act, lambda q=q, bank=bank: nc.scalar.activation(
                        out=sqr[:, q, :], in_=ps[:, bank, :], func=AF.Square),
                        reads=[B_ps[bank]], writes=[B_sqr[q]])
                    cx_.op(pe, lambda q=q, i=i: nc.tensor.matmul(
                        ps[:, stat_bank, :], lhsT=ones, rhs=sqr[:, q, :], start=(i == 0), stop=(i == KC - 1)),
                        reads=[B_cst, B_sqr[q]], writes=[B_ps[stat_bank]], inc=True)
            rstd_from_bank(stat_bank, 1)
            cx_.dma(sync, D_hst, hst[:], src[ti], reads=[B_src_tile], writes=[B_hst])
            for i in range(KC):
                f = state["tf"]; state["tf"] = 1 - f
                cx_.op(dve, lambda i=i, f=f: nc.vector.scalar_tensor_tensor(
                    out=tmpf[:, f, :], in0=mbuf[:, i, :], scalar=gpost_sb[:, l, i:i + 1],
                    in1=rstd[:, 1, :], op0=ALU.mult, op1=ALU.mult),
                    reads=[B_m, B_rstd[1], B_cst], writes=[B_tmpf[f]])
                cx_.op(dve, lambda i=i, f=f: nc.vector.tensor_tensor(
                    out=hst[:, i, :], in0=hst[:, i, :], in1=tmpf[:, f, :], op=ALU.add),
                    reads=[B_hst, B_tmpf[f]], writes=[B_hst])
            cx_.dma(sync, D_hst, dst[ti], hst[:], reads=[B_hst], writes=[B_dst_tile])

        wflat = lambda s: wbuf[:, s].rearrange("p a b -> p (a b)")

        def conv_layer(l, src, dst):
            with ExitStack() as ls:
                lsb = lambda name, shape, dtype: ls.enter_context(nc.sbuf_tensor(f"{name}_c{l}", shape, dtype))
                uT = lsb("uT", [128, 2, KC, T], BF16)
                c_sb = lsb("c_sb", [128, 2, T], F32)
                cxb = lsb("cxb", [128, 2, T + 2], F32)
                yb = lsb("yb", [128, 2, T], F32)
                szb = lsb("szb", [128, 2, T], F32)
                tail = lsb("tail", [128, KC, 2], F32)
                B_uT = [Buf("uT0"), Buf("uT1")]
                B_c = [Buf("c0"), Buf("c1")]
                B_cx = [Buf("cx0"), Buf("cx1")]
                B_y = [Buf("y0"), Buf("y1")]
                B_sz = [Buf("sz0"), Buf("sz1")]
                B_tail = Buf("tail")
                cx_.op(dve, lambda: nc.vector.memset(tail[:], 0.0), writes=[B_tail])
                rot = 0
                for stile in range(ntiles // 2):
                    tiles = [2 * stile, 2 * stile + 1]
                    for s in range(2):
                        build_uT(l, tiles[s], src, B_dram_h[tiles[s]], uT[:, s], B_uT[s])
                    for j in range(KC):
                        ws = load_w(w_in[l][j])
                        wf = wflat(ws)
                        for s in range(2):
                            banks = []
                            for blk in range(4):
                                bank = next_bank()
                                banks.append(bank)
                                mm_group(bank, [(wf[:, kc * T + blk * 128: kc * T + (blk + 1) * 128], uT[:, s, kc, :])
                                                for kc in range(KC)], reads=[B_w[ws], B_uT[s]])
                            bB, bC, bX, bZ = banks
                            r = rot; rot = 1 - rot
                            cx_.op(act, lambda r=r, bC=bC: nc.scalar.activation(out=c_sb[:, r, :], in_=ps[:, bC, :], func=AF.Copy),
                                   reads=[B_ps[bC]], writes=[B_c[r]])
                            cx_.op(act, lambda r=r, bZ=bZ: nc.scalar.activation(out=szb[:, r, :], in_=ps[:, bZ, :], func=AF.Silu),
                                   reads=[B_ps[bZ]], writes=[B_sz[r]])
                            cx_.op(dve, lambda r=r, j=j: nc.vector.tensor_copy(out=cxb[:, r, 0:2], in_=tail[:, j, :]),
                                   reads=[B_tail], writes=[B_cx[r]])
                            cx_.op(dve, lambda r=r, bX=bX: nc.vector.tensor_tensor(out=cxb[:, r, 2:T + 2], in0=c_sb[:, r, :], in1=ps[:, bX, :], op=ALU.mult),
                                   reads=[B_c[r], B_ps[bX]], writes=[B_cx[r]])
                            cx_.op(dve, lambda r=r, j=j: nc.vector.tensor_copy(out=tail[:, j, :], in_=cxb[:, r, T:T + 2]),
                                   reads=[B_cx[r]], writes=[B_tail])
                            cwl = cw_sb[:, l, :]
                            cx_.op(dve, lambda r=r, j=j, cwl=cwl: nc.vector.tensor_scalar(
                                out=yb[:, r, :], in0=cxb[:, r, 0:T], scalar1=cwl[:, j * 3 + 0: j * 3 + 1], scalar2=None, op0=ALU.mult),
                                reads=[B_cx[r], B_cst], writes=[B_y[r]])
                            for k in (1, 2):
                                cx_.op(dve, lambda r=r, j=j, k=k, cwl=cwl: nc.vector.scalar_tensor_tensor(
                                    out=yb[:, r, :], in0=cxb[:, r, k:T + k], scalar=cwl[:, j * 3 + k: j * 3 + k + 1],
                                    in1=yb[:, r, :], op0=ALU.mult, op1=ALU.add),
                                    reads=[B_cx[r], B_y[r], B_cst], writes=[B_y[r]])
                            cx_.op(dve, lambda r=r, bB=bB: nc.vector.tensor_tensor(out=yb[:, r, :], in0=yb[:, r, :], in1=ps[:, bB, :], op=ALU.mult),
                                   reads=[B_y[r], B_ps[bB]], writes=[B_y[r]])
                            cx_.op(dve, lambda r=r, j=j, s=s: nc.vector.tensor_tensor(out=gated[:, s, j, :], in0=yb[:, r, :], in1=szb[:, r, :], op=ALU.mult),
                                   reads=[B_y[r], B_sz[r]], writes=[B_gated[s]])
                    for s in range(2):
                        out_proj_postnorm(l, tiles[s], s, src, B_dram_h[tiles[s]], dst, B_dram_h[tiles[s]])
                cx_.barrier()

        def attn_layer(l, src, dst):
            B_scr = Buf("qkvz_scratch")
            with ExitStack() as ls:
                lsb = lambda name, shape, dtype: ls.enter_context(nc.sbuf_tensor(f"{name}_a{l}", shape, dtype))
                uT = lsb("uTa", [128, 2, KC, T], BF16)
                qo = lsb("qo", [128, 2, T], BF16)
                ko = lsb("ko", [128, 2, T], BF16)
                zo = lsb("zo", [128, 2, T], BF16)
                vo = lsb("vo", [128, 2, 4, DH], BF16)
                B_uT = [Buf("uTa0"), Buf("uTa1")]
                B_o = {n: [Buf(n + "0"), Buf(n + "1")] for n in "qkzv"}
                D_o = {n: [cx_.dsem(f"o{n}{l}_0"), cx_.dsem(f"o{n}{l}_1")] for n in "qkzv"}
                rot = 0
                scale = 1.0 / math.sqrt(DH)
                for stile in range(ntiles // 2):
                    tiles = [2 * stile, 2 * stile + 1]
                    for s in range(2):
                        build_uT(l, tiles[s], src, B_dram_h[tiles[s]], uT[:, s], B_uT[s])
                    for hd in range(H):
                        ws = load_w(w_in[l][hd])
                        wf = wflat(ws)
                        for s in range(2):
                            ti = tiles[s]
                            r = rot; rot = 1 - rot
                            tok = slice(ti * T, (ti + 1) * T)
                            bq = next_bank()
                            mm_group(bq, [(wf[:, kc * T + 0: kc * T + 128], uT[:, s, kc, :]) for kc in range(KC)], reads=[B_w[ws], B_uT[s]])
                            bk = next_bank()
                            mm_group(bk, [(wf[:, kc * T + 128: kc * T + 256], uT[:, s, kc, :]) for kc in range(KC)], reads=[B_w[ws], B_uT[s]])
                            bz = next_bank()
                            mm_group(bz, [(wf[:, kc * T + 384: kc * T + 512], uT[:, s, kc, :]) for kc in range(KC)], reads=[B_w[ws], B_uT[s]])
                            bv = next_bank()
                            for blk in range(4):
                                mm_group(bv, [(uT[:, s, kc, blk * 128:(blk + 1) * 128], wf[:, kc * T + 256: kc * T + 384]) for kc in range(KC)],
                                         reads=[B_w[ws], B_uT[s]], out_ap=ps[:, bv, blk * 128:(blk + 1) * 128])
                            cx_.op(act, lambda r=r, bq=bq: nc.scalar.activation(out=qo[:, r, :], in_=ps[:, bq, :], func=AF.Copy, scale=scale),
                                   reads=[B_ps[bq]], writes=[B_o["q"][r]])
                            cx_.op(act, lambda r=r, bz=bz: nc.scalar.activation(out=zo[:, r, :], in_=ps[:, bz, :], func=AF.Silu),
                                   reads=[B_ps[bz]], writes=[B_o["z"][r]])
                            cx_.op(dve, lambda r=r, bk=bk: nc.vector.tensor_copy(out=ko[:, r, :], in_=ps[:, bk, :]),
                                   reads=[B_ps[bk]], writes=[B_o["k"][r]])
                            cx_.op(dve, lambda r=r, bv=bv: nc.vector.tensor_copy(out=vo[:, r].rearrange("p a b -> p (a b)"), in_=ps[:, bv, :]),
                                   reads=[B_ps[bv]], writes=[B_o["v"][r]])
                            cx_.dma(sync, D_o["q"][r], QT[hd, :, tok], qo[:, r, :], reads=[B_o["q"][r]], writes=[B_scr])
                            cx_.dma(sync, D_o["k"][r], KT[hd, :, tok], ko[:, r, :], reads=[B_o["k"][r]], writes=[B_scr])
                            cx_.dma(sync, D_o["z"][r], ZT[hd, :, tok], zo[:, r, :], reads=[B_o["z"][r]], writes=[B_scr])
                            cx_.dma(sync, D_o["v"][r], VV[hd, :, ti * 4:(ti + 1) * 4, :], vo[:, r], reads=[B_o["v"][r]], writes=[B_scr])
                cx_.barrier()
            with ExitStack() as ls:
                lsb = lambda name, shape, dtype: ls.enter_context(nc.sbuf_tensor(f"{name}_b{l}", shape, dtype))
                ktb = lsb("ktb", [128, 2, S], BF16)
                vvb = lsb("vvb", [128, 2, S // 128, DH], BF16)
                qtb = lsb("qtb", [128, 2, T], BF16)
                ztb = lsb("ztb", [128, 2, T], BF16)
                NE, NSP, NW_ = 3, 4, 3
                eb = lsb("eb", [128, NE, T], F32)
                spb = lsb("spb", [128, NSP, T], BF16)
                Sb = lsb("Sb", [128, 3, T], BF16)
                Wb = lsb("Wb", [128, NW_, T], BF16)
                B_kv = [Buf("kv0"), Buf("kv1")]; D_kv = [cx_.dsem(f"kv{l}_0"), cx_.dsem(f"kv{l}_1")]
                B_e = [Buf(f"e{i}") for i in range(NE)]
                B_sp = [Buf(f"sp{i}") for i in range(NSP)]
                B_S = [Buf(f"S{i}") for i in range(3)]
                B_W = [Buf(f"W{i}") for i in range(NW_)]
                cnt = {"e": 0, "sp": 0, "S": 0, "W": 0, "kv": 0}
                state["ring"] = 6

                def rotn(name, n):
                    v = cnt[name]; cnt[name] = (v + 1) % n
                    return v

                for qt in range(ntiles):
                    gs = qt % 2
                    nkb = (qt + 1) * 4
                    pairs = []
                    for hd in range(H):
                        for idx, kb in enumerate(range(nkb - 1, -1, -1)):
                            pairs.append(dict(hd=hd, kb=kb, first=(idx == 0), last=(kb == 0),
                                              diag=(kb - qt * 4) if kb >= qt * 4 else None))
                    head_slot = {}
                    head_bank = {}

                    def load_head(hd):
                        sl = rotn("kv", 2)
                        head_slot[hd] = sl
                        nk = nkb * 128
                        cx_.dma(sync, D_kv[sl], ktb[:, sl, 0:nk], KT[hd, :, 0:nk], reads=[B_scr], writes=[B_kv[sl]])
                        cx_.dma(sync, D_kv[sl], vvb[:, sl, 0:nkb, :], VV[hd, :, 0:nkb, :], reads=[B_scr], writes=[B_kv[sl]])
                        cx_.dma(sync, D_kv[sl], qtb[:, sl, :], QT[hd, :, qt * T:(qt + 1) * T], reads=[B_scr], writes=[B_kv[sl]])
                        cx_.dma(sync, D_kv[sl], ztb[:, sl, :], ZT[hd, :, qt * T:(qt + 1) * T], reads=[B_scr], writes=[B_kv[sl]])

                    def stageA(p):
                        hd, kb = p["hd"], p["kb"]
                        sl = head_slot[hd]
                        bL = next_bank()
                        kslice = ktb[:, sl, kb * 128:(kb + 1) * 128]
                        mm_group(bL, [(kslice, qtb[:, sl, :])], reads=[B_kv[sl]])
                        ei = rotn("e", NE)
                        cx_.op(act, lambda: nc.scalar.activation(out=eb[:, ei, :], in_=ps[:, bL, :], func=AF.Exp),
                               reads=[B_ps[bL]], writes=[B_e[ei]])
                        si = rotn("sp", NSP)
                        cx_.op(act, lambda: nc.scalar.activation(out=spb[:, si, :], in_=eb[:, ei, :], func=AF.Ln, bias=1.0),
                               reads=[B_e[ei]], writes=[B_sp[si]])
                        if p["diag"] is not None:
                            dd = p["diag"]
                            cx_.op(dve, lambda: nc.vector.tensor_tensor(out=spb[:, si, :], in0=spb[:, si, :], in1=masks[dd], op=ALU.mult),
                                   reads=[B_sp[si], B_cst], writes=[B_sp[si]])
                        p["sp"] = si
                        p["kslice"] = kslice
                        p["sl"] = sl

                    def stageB(p, prev):
                        sl = p["sl"]
                        bX = next_bank()
                        si = p["sp"]
                        if p["first"]:
                            carry = None
                        elif prev["first"]:
                            carry = (spb[:, prev["sp"], :], B_sp[prev["sp"]])
                            p["Sidx"] = None
                        else:
                            sn = rotn("S", 3)
                            if prev.get("carry_ap") is None:
                                raise RuntimeError("carry chain broken")
                            pc_ap, pc_buf = prev["carry_ap"]
                            cx_.op(dve, lambda: nc.vector.tensor_tensor(out=Sb[:, sn, :], in0=pc_ap, in1=spb[:, prev["sp"], :], op=ALU.add),
                                   reads=[pc_buf, B_sp[prev["sp"]]], writes=[B_S[sn]])
                            carry = (Sb[:, sn, :], B_S[sn])
                        p["carry_ap"] = carry
                        mms = [(p["kslice"], qtb[:, sl, :]), (negtri, spb[:, si, :])]
                        rd = [B_kv[sl], B_cst, B_sp[si]]
                        if carry is not None:
                            mms.append((negones, carry[0]))
                            rd.append(carry[1])
                        mm_group(bX, mms, reads=rd)
                        wi = rotn("W", NW_)
                        cx_.op(act, lambda: nc.scalar.activation(out=Wb[:, wi, :], in_=ps[:, bX, :], func=AF.Exp),
                               reads=[B_ps[bX]], writes=[B_W[wi]])
                        if p["diag"] is not None:
                            dd = p["diag"]
                            cx_.op(dve, lambda: nc.vector.tensor_tensor(out=Wb[:, wi, :], in0=Wb[:, wi, :], in1=masks[dd], op=ALU.mult),
                                   reads=[B_W[wi], B_cst], writes=[B_W[wi]])
                        p["W"] = wi

                    def stageC(p):
                        hd, kb, sl = p["hd"], p["kb"], p["sl"]
                        bO = 6 + (hd % 2)
                        wi = p["W"]
                        cx_.op(pe, lambda: nc.tensor.matmul(ps[:, bO, :], lhsT=vvb[:, sl, kb, :], rhs=Wb[:, wi, :],
                                                            start=p["first"], stop=p["last"]),
                               reads=[B_kv[sl], B_W[wi]], writes=[B_ps[bO]], inc=True)
                        if p["last"]:
                            cx_.op(dve, lambda: nc.vector.tensor_tensor(out=gated[:, gs, hd, :], in0=ztb[:, sl, :], in1=ps[:, bO, :], op=ALU.mult),
                                   reads=[B_kv[sl], B_ps[bO]], writes=[B_gated[gs]])

                    n = len(pairs)
                    load_head(0)
                    load_head(1)
                    for step in range(n + 2):
                        if step < n:
                            stageA(pairs[step])
                        if 0 <= step - 1 < n:
                            stageB(pairs[step - 1], pairs[step - 2] if step - 2 >= 0 else None)
                        if 0 <= step - 2 < n:
                            pc = pairs[step - 2]
                            stageC(pc)
                            if pc["last"] and pc["hd"] + 2 < H:
                                load_head(pc["hd"] + 2)
                    out_proj_postnorm(l, qt, gs, src, B_dram_h[qt], dst, B_dram_h[qt])
                state["ring"] = 8
                cx_.barrier()

        for l in range(depth):
            src = xT if l == 0 else hbuf
            dst = yT if l == depth - 1 else hbuf
            if l % 2 == 0:
                conv_layer(l, src, dst)
            else:
                attn_layer(l, src, dst)
        cx_.barrier()
    return nc


def _tile_x(xb):
    a = xb.reshape(NT, T, KC, 128)
    return np.ascontiguousarray(a.transpose(0, 3, 2, 1))


def _untile_y(yt):
    return np.ascontiguousarray(yt.transpose(0, 3, 2, 1)).reshape(S, D)


def _tile_w_in(w):
    a = w.reshape(KC, 128, 4, 16, 128)
    a = a.transpose(3, 1, 0, 2, 4)
    return np.ascontiguousarray(a).reshape(16, 128, 8, 1024)


def _tile_w_out(w):
    a = w.reshape(KC, 128, 4, 512)
    a = a.transpose(2, 1, 0, 3)
    return np.ascontiguousarray(a).reshape(4, 128, 8, 1024)


def _gvec(g):
    return np.ascontiguousarray(g.reshape(KC, 128).T)


def _cwt(cwk):
    a = cwk.reshape(3, KC, 128).transpose(2, 1, 0)
    return np.ascontiguousarray(a).reshape(128, KC * 3)


def _consts():
    c = np.zeros((128, 3 * 128 + 4 * T), np.float32)
    c[:, 0:128] = 1.0
    j = np.arange(128)[:, None]
    k = np.arange(128)[None, :]
    c[:, 128:256] = np.where(j >= k, -1.0, 0.0)
    c[:, 256:384] = -1.0
    col = np.arange(T)[None, :]
    for d in range(4):
        c[:, 384 + d * T: 384 + (d + 1) * T] = (col > 128 * d + j).astype(np.float32)
    return c


_NC_CACHE = {}


def kernel(**inputs):
    x = np.asarray(inputs["x"], dtype=np.float32)
    if "nc" not in _NC_CACHE:
        _NC_CACHE["nc"] = build_program()
    nc = _NC_CACHE["nc"]
    shared = {"cmat": _consts()}
    for l in range(DEPTH):
        if l % 2 == 0:
            shared[f"w_in{l}"] = _tile_w_in(np.asarray(inputs[f"conv_w_in_{l}"], np.float32))
            shared[f"w_out{l}"] = _tile_w_out(np.asarray(inputs[f"conv_w_out_{l}"], np.float32))
            shared[f"cw{l}"] = _cwt(np.asarray(inputs[f"conv_w_{l}"], np.float32))
        else:
            shared[f"w_in{l}"] = _tile_w_in(np.asarray(inputs[f"sb_w_in_{l}"], np.float32))
            shared[f"w_out{l}"] = _tile_w_out(np.asarray(inputs[f"sb_w_out_{l}"], np.float32))
        shared[f"g_pre{l}"] = _gvec(np.asarray(inputs[f"ln_pre_{l}"], np.float32))
        shared[f"g_post{l}"] = _gvec(np.asarray(inputs[f"ln_post_{l}"], np.float32))
    xts = [_tile_x(x[b]) for b in range(B)]
    in_maps = []
    for c in range(NCORES):
        m = dict(shared)
        m["xT"] = xts[c // 2]
        in_maps.append(m)
    res = run_bass_kernel_spmd(nc, in_maps, core_ids=list(range(NCORES)))
    out = np.stack([_untile_y(np.asarray(res.results[2 * b]["yT"])) for b in range(B)], axis=0)
    return out.astype(np.float32)
```

```python
import math
import numpy as np
import concourse.bass as bass
import concourse.mybir as mybir
from concourse.bass_utils import run_bass_kernel_spmd

F32 = mybir.dt.float32
BF16 = mybir.dt.bfloat16
AF = mybir.ActivationFunctionType
ALU = mybir.AluOpType

D = 2048
B = 4
S = 4096
DEPTH = 4
H = 16
DH = 128
T = 512
NT = S // T
KC = D // 128
EPS = 1e-6
NCORES = 8


class Sem:
    def __init__(self, nc, stack, name):
        self.h = stack.enter_context(nc.semaphore(name))
        self.name = name
        self.issued = 0


class Buf:
    def __init__(self, name):
        self.name = name
        self.w = {}
        self.r = {}


class Eng:
    def __init__(self, nc, stack, name, handle, is_queue=False):
        self.name = name
        self.h = handle
        self.sem = None if is_queue else Sem(nc, stack, "s_" + name)
        self.seen = {}

    def wait(self, sem, val):
        if val <= 0:
            return
        if self.seen.get(sem, 0) >= val:
            return
        self.h.wait_ge(sem.h, val)
        self.seen[sem] = val


class Ctx:
    def __init__(self, nc, stack):
        self.nc = nc
        self.stack = stack
        self.pe = Eng(nc, stack, "pe", nc.tensor)
        self.act = Eng(nc, stack, "act", nc.scalar)
        self.dve = Eng(nc, stack, "dve", nc.vector)
        self.sync = Eng(nc, stack, "sync", nc.sync, is_queue=True)
        self.pool = Eng(nc, stack, "pool", nc.gpsimd, is_queue=True)
        self.engs = [self.pe, self.act, self.dve, self.sync, self.pool]
        self.sems = [self.pe.sem, self.act.sem, self.dve.sem]
        self.nbuf = 0

    def buf(self, name):
        return Buf(name)

    def dsem(self, name):
        s = Sem(self.nc, self.stack, "d_" + name)
        self.sems.append(s)
        return s

    def _waits(self, eng, reads, writes):
        need = {}
        for b in reads:
            for s, v in b.w.items():
                need[s] = max(need.get(s, 0), v)
        for b in writes:
            for s, v in b.w.items():
                if s is eng.sem:
                    continue
                need[s] = max(need.get(s, 0), v)
            for s, v in b.r.items():
                if s is eng.sem:
                    continue
                need[s] = max(need.get(s, 0), v)
        for s, v in need.items():
            eng.wait(s, v)

    def op(self, eng, fn, reads=(), writes=(), inc=True):
        self._waits(eng, reads, writes)
        ins = fn()
        if inc:
            ins.then_inc(eng.sem.h, 1)
            eng.sem.issued += 1
            v = eng.sem.issued
            for b in writes:
                b.w[eng.sem] = v
            for b in reads:
                b.r[eng.sem] = v
        return ins

    def dma(self, q, sem, out, in_, reads=(), writes=(), **kw):
        self._waits(q, reads, writes)
        q.h.dma_start(out=out, in_=in_, **kw).then_inc(sem.h, 16)
        sem.issued += 16
        for b in writes:
            b.w[sem] = sem.issued
        for b in reads:
            b.r[sem] = sem.issued

    def barrier(self):
        for e in self.engs:
            for s in self.sems:
                e.wait(s, s.issued)


def build_program(depth=DEPTH, ntiles=NT):
    from contextlib import ExitStack
    nc = bass.Bass("TRN2", target_bir_lowering=False)
    dt = nc.dram_tensor
    xT = dt("xT", [NT, 128, KC, T], F32, kind="ExternalInput").ap()
    yT = dt("yT", [NT, 128, KC, T], F32, kind="ExternalOutput").ap()
    w_in, w_out, g_pre, g_post, cw = [], [], [], [], []
    for l in range(depth):
        w_in.append(dt(f"w_in{l}", [16, 128, 8, 1024], F32, kind="ExternalInput").ap())
        w_out.append(dt(f"w_out{l}", [4, 128, 8, 1024], F32, kind="ExternalInput").ap())
        g_pre.append(dt(f"g_pre{l}", [128, KC], F32, kind="ExternalInput").ap())
        g_post.append(dt(f"g_post{l}", [128, KC], F32, kind="ExternalInput").ap())
        if l % 2 == 0:
            cw.append(dt(f"cw{l}", [128, KC * 3], F32, kind="ExternalInput").ap())
        else:
            cw.append(None)
    cmat = dt("cmat", [128, 3 * 128 + 4 * T], F32, kind="ExternalInput").ap()
    hbuf = dt("hbuf", [NT, 128, KC, T], F32, kind="Internal").ap()
    QT = dt("QT", [H, 128, S], BF16, kind="Internal").ap()
    KT = dt("KT", [H, 128, S], BF16, kind="Internal").ap()
    ZT = dt("ZT", [H, 128, S], BF16, kind="Internal").ap()
    VV = dt("VV", [H, 128, S // 128, DH], BF16, kind="Internal").ap()

    with ExitStack() as st:
        cx_ = Ctx(nc, st)
        pe, act, dve, sync, pool = cx_.pe, cx_.act, cx_.dve, cx_.sync, cx_.pool
        sb = lambda name, shape, dtype: st.enter_context(nc.sbuf_tensor(name, shape, dtype))

        hst = sb("hst", [128, KC, T], F32)
        gated = sb("gated", [128, 2, KC, T], BF16)
        wbuf = sb("wbuf", [128, 2, 8, 1024], BF16)
        mbuf = sb("mbuf", [128, KC, T], F32)
        cst = sb("cst", [128, 3 * 128 + 4 * T], BF16)
        gpre_sb = sb("gpre_sb", [128, depth, KC], F32)
        gpost_sb = sb("gpost_sb", [128, depth, KC], F32)
        cw_sb = sb("cw_sb", [128, depth, KC * 3], F32)
        rstd = sb("rstd", [128, 2, T], F32)
        lnt = sb("lnt", [128, T], F32)
        sqr = sb("sqr", [128, 2, T], BF16)
        tmpf = sb("tmpf", [128, 2, T], F32)
        ps = st.enter_context(nc.psum_tensor("ps", [128, 8, T], F32))

        B_hst = Buf("hst"); D_hst = cx_.dsem("hst")
        B_gated = [Buf("gated0"), Buf("gated1")]
        B_w = [Buf("w0"), Buf("w1")]; D_w = [cx_.dsem("w0"), cx_.dsem("w1")]
        B_m = Buf("m")
        B_cst = Buf("cst"); D_cst = cx_.dsem("cst")
        B_rstd = [Buf("rstd0"), Buf("rstd1")]
        B_lnt = Buf("lnt")
        B_sqr = [Buf("sqr0"), Buf("sqr1")]
        B_tmpf = [Buf("tmpf0"), Buf("tmpf1")]
        B_ps = [Buf(f"ps{i}") for i in range(8)]
        B_dram_h = [Buf(f"h{t}") for t in range(NT)]
        state = {"bank": 0, "w": 0, "sq": 0, "tf": 0, "ring": 8}

        def next_bank():
            b = state["bank"] % state["ring"]; state["bank"] = (b + 1) % state["ring"]
            return b

        ones = cst[:, 0:128]
        negtri = cst[:, 128:256]
        negones = cst[:, 256:384]
        masks = [cst[:, 384 + d * T: 384 + (d + 1) * T] for d in range(4)]

        CW = 3 * 128 + 4 * T
        for c0 in range(0, CW, 1216):
            cx_.dma(pool, D_cst, cst[:, c0:c0 + 1216], cmat[:, c0:c0 + 1216], writes=[B_cst])
        for l in range(depth):
            cx_.dma(sync, D_cst, gpre_sb[:, l, :], g_pre[l][:], writes=[B_cst])
            cx_.dma(sync, D_cst, gpost_sb[:, l, :], g_post[l][:], writes=[B_cst])
            if cw[l] is not None:
                cx_.dma(sync, D_cst, cw_sb[:, l, :], cw[l][:], writes=[B_cst])

        def load_w(src_tile):
            s = state["w"]; state["w"] = 1 - s
            cx_.dma(pool, D_w[s], wbuf[:, s], src_tile, writes=[B_w[s]])
            return s

        def mm_group(bank, pairs, reads, final_inc=True, out_ap=None):
            o = ps[:, bank, :] if out_ap is None else out_ap
            n = len(pairs)
            for k, (l_, r_) in enumerate(pairs):
                last = (k == n - 1)
                cx_.op(pe, lambda l_=l_, r_=r_, k=k, last=last: nc.tensor.matmul(
                    o, lhsT=l_, rhs=r_, start=(k == 0), stop=last),
                    reads=reads, writes=[B_ps[bank]], inc=(last and final_inc))

        def rstd_from_bank(bank, slot):
            cx_.op(act, lambda: nc.scalar.activation(out=lnt[:], in_=ps[:, bank, :], func=AF.Ln,
                                                     scale=1.0 / D, bias=EPS),
                   reads=[B_ps[bank]], writes=[B_lnt])
            cx_.op(act, lambda: nc.scalar.activation(out=rstd[:, slot, :], in_=lnt[:], func=AF.Exp,
                                                     scale=-0.5),
                   reads=[B_lnt], writes=[B_rstd[slot]])

        def build_uT(l, ti, src, B_src_tile, uT_ap, B_uT):
            cx_.dma(sync, D_hst, hst[:], src[ti], reads=[B_src_tile], writes=[B_hst])
            cx_.op(act, lambda: nc.scalar.activation(out=uT_ap, in_=hst[:], func=AF.Square),
                   reads=[B_hst], writes=[B_uT])
            bank = next_bank()
            mm_group(bank, [(ones, uT_ap[:, kc, :]) for kc in range(KC)], reads=[B_cst, B_uT])
            rstd_from_bank(bank, 0)
            for kc in range(KC):
                cx_.op(dve, lambda kc=kc: nc.vector.scalar_tensor_tensor(
                    out=uT_ap[:, kc, :], in0=hst[:, kc, :], scalar=gpre_sb[:, l, kc:kc + 1],
                    in1=rstd[:, 0, :], op0=ALU.mult, op1=ALU.mult),
                    reads=[B_hst, B_rstd[0], B_cst], writes=[B_uT])

        def out_proj_postnorm(l, ti, gslot, src, B_src_tile, dst, B_dst_tile):
            stat_bank = next_bank()
            for ig in range(4):
                ws = load_w(w_out[l][ig])
                for ii in range(4):
                    i = ig * 4 + ii
                    bank = next_bank()
                    if bank == stat_bank:
                        bank = next_bank()
                    mm_group(bank, [(wbuf[:, ws].rearrange("p a b -> p (a b)")[:, j * T + ii * 128: j * T + (ii + 1) * 128],
                                     gated[:, gslot, j, :]) for j in range(KC)],
                             reads=[B_w[ws], B_gated[gslot]])
                    cx_.op(act, lambda i=i, bank=bank: nc.scalar.activation(
                        out=mbuf[:, i, :], in_=ps[:, bank, :], func=AF.Copy),
                        reads=[B_ps[bank]], writes=[B_m])
                    q = state["sq"]; state["sq"] = 1 - q
                    cx_.op(act, lambda q=q, bank=bank: nc.scalar.activation(
                        out=sqr[:, q, :], in_=ps[:, bank, :], func=AF.Square),
                        reads=[B_ps[bank]], writes=[B_sqr[q]])
                    cx_.op(pe, lambda q=q, i=i: nc.tensor.matmul(
                        ps[:, stat_bank, :], lhsT=ones, rhs=sqr[:, q, :], start=(i == 0), stop=(i == KC - 1)),
                        reads=[B_cst, B_sqr[q]], writes=[B_ps[stat_bank]], inc=True)
            rstd_from_bank(stat_bank, 1)
            cx_.dma(sync, D_hst, hst[:], src[ti], reads=[B_src_tile], writes=[B_hst])
            for i in range(KC):
                f = state["tf"]; state["tf"] = 1 - f
                cx_.op(dve, lambda i=i, f=f: nc.vector.scalar_tensor_tensor(
                    out=tmpf[:, f, :], in0=mbuf[:, i, :], scalar=gpost_sb[:, l, i:i + 1],
                    in1=rstd[:, 1, :], op0=ALU.mult, op1=ALU.mult),
                    reads=[B_m, B_rstd[1], B_cst], writes=[B_tmpf[f]])
                cx_.op(dve, lambda i=i, f=f: nc.vector.tensor_tensor(
                    out=hst[:, i, :], in0=hst[:, i, :], in1=tmpf[:, f, :], op=ALU.add),
                    reads=[B_hst, B_tmpf[f]], writes=[B_hst])
            cx_.dma(sync, D_hst, dst[ti], hst[:], reads=[B_hst], writes=[B_dst_tile])

        wflat = lambda s: wbuf[:, s].rearrange("p a b -> p (a b)")

        def conv_layer(l, src, dst):
            with ExitStack() as ls:
                lsb = lambda name, shape, dtype: ls.enter_context(nc.sbuf_tensor(f"{name}_c{l}", shape, dtype))
                uT = lsb("uT", [128, 2, KC, T], BF16)
                c_sb = lsb("c_sb", [128, 2, T], F32)
                cxb = lsb("cxb", [128, 2, T + 2], F32)
                yb = lsb("yb", [128, 2, T], F32)
                szb = lsb("szb", [128, 2, T], F32)
                tail = lsb("tail", [128, KC, 2], F32)
                B_uT = [Buf("uT0"), Buf("uT1")]
                B_c = [Buf("c0"), Buf("c1")]
                B_cx = [Buf("cx0"), Buf("cx1")]
                B_y = [Buf("y0"), Buf("y1")]
                B_sz = [Buf("sz0"), Buf("sz1")]
                B_tail = Buf("tail")
                cx_.op(dve, lambda: nc.vector.memset(tail[:], 0.0), writes=[B_tail])
                rot = 0
                for stile in range(ntiles // 2):
                    tiles = [2 * stile, 2 * stile + 1]
                    for s in range(2):
                        build_uT(l, tiles[s], src, B_dram_h[tiles[s]], uT[:, s], B_uT[s])
                    for j in range(KC):
                        ws = load_w(w_in[l][j])
                        wf = wflat(ws)
                        for s in range(2):
                            banks = []
                            for blk in range(4):
                                bank = next_bank()
                                banks.append(bank)
                                mm_group(bank, [(wf[:, kc * T + blk * 128: kc * T + (blk + 1) * 128], uT[:, s, kc, :])
                                                for kc in range(KC)], reads=[B_w[ws], B_uT[s]])
                            bB, bC, bX, bZ = banks
                            r = rot; rot = 1 - rot
                            cx_.op(act, lambda r=r, bC=bC: nc.scalar.activation(out=c_sb[:, r, :], in_=ps[:, bC, :], func=AF.Copy),
                                   reads=[B_ps[bC]], writes=[B_c[r]])
                            cx_.op(act, lambda r=r, bZ=bZ: nc.scalar.activation(out=szb[:, r, :], in_=ps[:, bZ, :], func=AF.Silu),
                                   reads=[B_ps[bZ]], writes=[B_sz[r]])
                            cx_.op(dve, lambda r=r, j=j: nc.vector.tensor_copy(out=cxb[:, r, 0:2], in_=tail[:, j, :]),
                                   reads=[B_tail], writes=[B_cx[r]])
                            cx_.op(dve, lambda r=r, bX=bX: nc.vector.tensor_tensor(out=cxb[:, r, 2:T + 2], in0=c_sb[:, r, :], in1=ps[:, bX, :], op=ALU.mult),
                                   reads=[B_c[r], B_ps[bX]], writes=[B_cx[r]])
                            cx_.op(dve, lambda r=r, j=j: nc.vector.tensor_copy(out=tail[:, j, :], in_=cxb[:, r, T:T + 2]),
                                   reads=[B_cx[r]], writes=[B_tail])
                            cwl = cw_sb[:, l, :]
                            cx_.op(dve, lambda r=r, j=j, cwl=cwl: nc.vector.tensor_scalar(
                                out=yb[:, r, :], in0=cxb[:, r, 0:T], scalar1=cwl[:, j * 3 + 0: j * 3 + 1], scalar2=None, op0=ALU.mult),
                                reads=[B_cx[r], B_cst], writes=[B_y[r]])
                            for k in (1, 2):
                                cx_.op(dve, lambda r=r, j=j, k=k, cwl=cwl: nc.vector.scalar_tensor_tensor(
                                    out=yb[:, r, :], in0=cxb[:, r, k:T + k], scalar=cwl[:, j * 3 + k: j * 3 + k + 1],
                                    in1=yb[:, r, :], op0=ALU.mult, op1=ALU.add),
                                    reads=[B_cx[r], B_y[r], B_cst], writes=[B_y[r]])
                            cx_.op(dve, lambda r=r, bB=bB: nc.vector.tensor_tensor(out=yb[:, r, :], in0=yb[:, r, :], in1=ps[:, bB, :], op=ALU.mult),
                                   reads=[B_y[r], B_ps[bB]], writes=[B_y[r]])
                            cx_.op(dve, lambda r=r, j=j, s=s: nc.vector.tensor_tensor(out=gated[:, s, j, :], in0=yb[:, r, :], in1=szb[:, r, :], op=ALU.mult),
                                   reads=[B_y[r], B_sz[r]], writes=[B_gated[s]])
                    for s in range(2):
                        out_proj_postnorm(l, tiles[s], s, src, B_dram_h[tiles[s]], dst, B_dram_h[tiles[s]])
                cx_.barrier()

        def attn_layer(l, src, dst):
            B_scr = Buf("qkvz_scratch")
            with ExitStack() as ls:
                lsb = lambda name, shape, dtype: ls.enter_context(nc.sbuf_tensor(f"{name}_a{l}", shape, dtype))
                uT = lsb("uTa", [128, 2, KC, T], BF16)
                qo = lsb("qo", [128, 2, T], BF16)
                ko = lsb("ko", [128, 2, T], BF16)
                zo = lsb("zo", [128, 2, T], BF16)
                vo = lsb("vo", [128, 2, 4, DH], BF16)
                B_uT = [Buf("uTa0"), Buf("uTa1")]
                B_o = {n: [Buf(n + "0"), Buf(n + "1")] for n in "qkzv"}
                D_o = {n: [cx_.dsem(f"o{n}{l}_0"), cx_.dsem(f"o{n}{l}_1")] for n in "qkzv"}
                rot = 0
                scale = 1.0 / math.sqrt(DH)
                for stile in range(ntiles // 2):
                    tiles = [2 * stile, 2 * stile + 1]
                    for s in range(2):
                        build_uT(l, tiles[s], src, B_dram_h[tiles[s]], uT[:, s], B_uT[s])
                    for hd in range(H):
                        ws = load_w(w_in[l][hd])
                        wf = wflat(ws)
                        for s in range(2):
                            ti = tiles[s]
                            r = rot; rot = 1 - rot
                            tok = slice(ti * T, (ti + 1) * T)
                            bq = next_bank()
                            mm_group(bq, [(wf[:, kc * T + 0: kc * T + 128], uT[:, s, kc, :]) for kc in range(KC)], reads=[B_w[ws], B_uT[s]])
                            bk = next_bank()
                            mm_group(bk, [(wf[:, kc * T + 128: kc * T + 256], uT[:, s, kc, :]) for kc in range(KC)], reads=[B_w[ws], B_uT[s]])
                            bz = next_bank()
                            mm_group(bz, [(wf[:, kc * T + 384: kc * T + 512], uT[:, s, kc, :]) for kc in range(KC)], reads=[B_w[ws], B_uT[s]])
                            bv = next_bank()
                            for blk in range(4):
                                mm_group(bv, [(uT[:, s, kc, blk * 128:(blk + 1) * 128], wf[:, kc * T + 256: kc * T + 384]) for kc in range(KC)],
                                         reads=[B_w[ws], B_uT[s]], out_ap=ps[:, bv, blk * 128:(blk + 1) * 128])
                            cx_.op(act, lambda r=r, bq=bq: nc.scalar.activation(out=qo[:, r, :], in_=ps[:, bq, :], func=AF.Copy, scale=scale),
                                   reads=[B_ps[bq]], writes=[B_o["q"][r]])
                            cx_.op(act, lambda r=r, bz=bz: nc.scalar.activation(out=zo[:, r, :], in_=ps[:, bz, :], func=AF.Silu),
                                   reads=[B_ps[bz]], writes=[B_o["z"][r]])
                            cx_.op(dve, lambda r=r, bk=bk: nc.vector.tensor_copy(out=ko[:, r, :], in_=ps[:, bk, :]),
                                   reads=[B_ps[bk]], writes=[B_o["k"][r]])
                            cx_.op(dve, lambda r=r, bv=bv: nc.vector.tensor_copy(out=vo[:, r].rearrange("p a b -> p (a b)"), in_=ps[:, bv, :]),
                                   reads=[B_ps[bv]], writes=[B_o["v"][r]])
                            cx_.dma(sync, D_o["q"][r], QT[hd, :, tok], qo[:, r, :], reads=[B_o["q"][r]], writes=[B_scr])
                            cx_.dma(sync, D_o["k"][r], KT[hd, :, tok], ko[:, r, :], reads=[B_o["k"][r]], writes=[B_scr])
                            cx_.dma(sync, D_o["z"][r], ZT[hd, :, tok], zo[:, r, :], reads=[B_o["z"][r]], writes=[B_scr])
                            cx_.dma(sync, D_o["v"][r], VV[hd, :, ti * 4:(ti + 1) * 4, :], vo[:, r], reads=[B_o["v"][r]], writes=[B_scr])
                cx_.barrier()
            with ExitStack() as ls:
                lsb = lambda name, shape, dtype: ls.enter_context(nc.sbuf_tensor(f"{name}_b{l}", shape, dtype))
                ktb = lsb("ktb", [128, 2, S], BF16)
                vvb = lsb("vvb", [128, 2, S // 128, DH], BF16)
                qtb = lsb("qtb", [128, 2, T], BF16)
                ztb = lsb("ztb", [128, 2, T], BF16)
                NE, NSP, NW_ = 2, 3, 2
                eb = lsb("eb", [128, NE, 2, T], F32)
                spb = lsb("spb", [128, NSP, 2, T], BF16)
                Sb = lsb("Sb", [128, 3, T], BF16)
                tsb = lsb("tsb", [128, 2, T], BF16)
                Wb = lsb("Wb", [128, NW_, 2, T], BF16)
                B_kv = [Buf("kv0"), Buf("kv1")]; D_kv = [cx_.dsem(f"kv{l}_0"), cx_.dsem(f"kv{l}_1")]
                B_e = [Buf(f"e{i}") for i in range(NE)]
                B_sp = [Buf(f"sp{i}") for i in range(NSP)]
                B_S = [Buf(f"S{i}") for i in range(3)]
                B_ts = [Buf("ts0"), Buf("ts1")]
                B_W = [Buf(f"W{i}") for i in range(NW_)]
                cnt = {"e": 0, "sp": 0, "S": 0, "W": 0, "kv": 0, "pair": 0, "ts": 0}
                state["ring"] = 6

                def rotn(name, n):
                    v = cnt[name]; cnt[name] = (v + 1) % n
                    return v

                def next_pair():
                    g = rotn("pair", 3)
                    return 2 * g

                for qt in range(ntiles):
                    gs = qt % 2
                    nkb = (qt + 1) * 4
                    dbl = []
                    for hd in range(H):
                        for idx in range(nkb // 2):
                            khi = nkb - 1 - 2 * idx
                            dbl.append(dict(hd=hd, khi=khi, klo=khi - 1, first=(idx == 0), last=(khi - 1 == 0),
                                            diag=((0 if khi - qt * 4 == 3 else 1) if khi >= qt * 4 else None)))
                    head_slot = {}

                    def load_head(hd):
                        sl = rotn("kv", 2)
                        head_slot[hd] = sl
                        nk = nkb * 128
                        cx_.dma(sync, D_kv[sl], ktb[:, sl, 0:nk], KT[hd, :, 0:nk], reads=[B_scr], writes=[B_kv[sl]])
                        cx_.dma(sync, D_kv[sl], vvb[:, sl, 0:nkb, :], VV[hd, :, 0:nkb, :], reads=[B_scr], writes=[B_kv[sl]])
                        cx_.dma(sync, D_kv[sl], qtb[:, sl, :], QT[hd, :, qt * T:(qt + 1) * T], reads=[B_scr], writes=[B_kv[sl]])
                        cx_.dma(sync, D_kv[sl], ztb[:, sl, :], ZT[hd, :, qt * T:(qt + 1) * T], reads=[B_scr], writes=[B_kv[sl]])

                    def kslice(p, kb):
                        return ktb[:, p["sl"], kb * 128:(kb + 1) * 128]

                    def stageA1(p):
                        sl = head_slot[p["hd"]]
                        p["sl"] = sl
                        b0 = next_pair()
                        mm_group(b0, [(kslice(p, p["khi"]), qtb[:, sl, :])], reads=[B_kv[sl]])
                        mm_group(b0 + 1, [(kslice(p, p["klo"]), qtb[:, sl, :])], reads=[B_kv[sl]])
                        ei = rotn("e", NE)
                        cx_.op(act, lambda: nc.scalar.activation(out=eb[:, ei], in_=ps[:, b0:b0 + 2, :], func=AF.Exp),
                               reads=[B_ps[b0], B_ps[b0 + 1]], writes=[B_e[ei]])
                        p["e"] = ei

                    def stageA2(p):
                        ei = p["e"]
                        si = rotn("sp", NSP)
                        cx_.op(act, lambda: nc.scalar.activation(out=spb[:, si], in_=eb[:, ei], func=AF.Ln, bias=1.0),
                               reads=[B_e[ei]], writes=[B_sp[si]])
                        if p["diag"] is not None:
                            mk = cst[:, 384 + p["diag"] * 2 * T: 384 + (p["diag"] + 1) * 2 * T]
                            cx_.op(dve, lambda: nc.vector.tensor_tensor(out=spb[:, si].rearrange("p a b -> p (a b)"),
                                                                        in0=spb[:, si].rearrange("p a b -> p (a b)"), in1=mk, op=ALU.mult),
                                   reads=[B_sp[si], B_cst], writes=[B_sp[si]])
                        p["sp"] = si

                    def stageB(p, prev):
                        sl, si = p["sl"], p["sp"]
                        carry = None if p["first"] else prev["S_next"]
                        b0 = next_pair()
                        mms = [(kslice(p, p["khi"]), qtb[:, sl, :]), (negtri, spb[:, si, 0, :])]
                        rd = [B_kv[sl], B_cst, B_sp[si]]
                        if carry is not None:
                            mms.append((negones, carry[0])); rd.append(carry[1])
                        mm_group(b0, mms, reads=rd)
                        mms = [(kslice(p, p["klo"]), qtb[:, sl, :]), (negtri, spb[:, si, 1, :]), (negones, spb[:, si, 0, :])]
                        if carry is not None:
                            mms.append((negones, carry[0]))
                        mm_group(b0 + 1, mms, reads=rd)
                        wi = rotn("W", NW_)
                        cx_.op(act, lambda: nc.scalar.activation(out=Wb[:, wi], in_=ps[:, b0:b0 + 2, :], func=AF.Exp),
                               reads=[B_ps[b0], B_ps[b0 + 1]], writes=[B_W[wi]])
                        if p["diag"] is not None:
                            mk = cst[:, 384 + p["diag"] * 2 * T: 384 + (p["diag"] + 1) * 2 * T]
                            cx_.op(dve, lambda: nc.vector.tensor_tensor(out=Wb[:, wi].rearrange("p a b -> p (a b)"),
                                                                        in0=Wb[:, wi].rearrange("p a b -> p (a b)"), in1=mk, op=ALU.mult),
                                   reads=[B_W[wi], B_cst], writes=[B_W[wi]])
                        p["W"] = wi
                        if not p["last"]:
                            sn = rotn("S", 3)
                            if carry is None:
                                cx_.op(dve, lambda: nc.vector.tensor_tensor(out=Sb[:, sn, :], in0=spb[:, si, 0, :], in1=spb[:, si, 1, :], op=ALU.add),
                                       reads=[B_sp[si]], writes=[B_S[sn]])
                            else:
                                ti_ = rotn("ts", 2)
                                cx_.op(dve, lambda: nc.vector.tensor_tensor(out=tsb[:, ti_, :], in0=spb[:, si, 0, :], in1=spb[:, si, 1, :], op=ALU.add),
                                       reads=[B_sp[si]], writes=[B_ts[ti_]])
                                cx_.op(dve, lambda: nc.vector.tensor_tensor(out=Sb[:, sn, :], in0=tsb[:, ti_, :], in1=carry[0], op=ALU.add),
                                       reads=[B_ts[ti_], carry[1]], writes=[B_S[sn]])
                            p["S_next"] = (Sb[:, sn, :], B_S[sn])

                    def stageC(p):
                        hd, sl, wi = p["hd"], p["sl"], p["W"]
                        bO = 6 + (hd % 2)
                        cx_.op(pe, lambda: nc.tensor.matmul(ps[:, bO, :], lhsT=vvb[:, sl, p["khi"], :], rhs=Wb[:, wi, 0, :],
                                                            start=p["first"], stop=False),
                               reads=[B_kv[sl], B_W[wi]], writes=[B_ps[bO]], inc=False)
                        cx_.op(pe, lambda: nc.tensor.matmul(ps[:, bO, :], lhsT=vvb[:, sl, p["klo"], :], rhs=Wb[:, wi, 1, :],
                                                            start=False, stop=p["last"]),
                               reads=[B_kv[sl], B_W[wi]], writes=[B_ps[bO]], inc=True)
                        if p["last"]:
                            cx_.op(dve, lambda: nc.vector.tensor_tensor(out=gated[:, gs, hd, :], in0=ztb[:, sl, :], in1=ps[:, bO, :], op=ALU.mult),
                                   reads=[B_kv[sl], B_ps[bO]], writes=[B_gated[gs]])

                    n = len(dbl)
                    load_head(0)
                    load_head(1)
                    for step in range(n + 3):
                        if 0 <= step - 3 < n:
                            pc = dbl[step - 3]
                            stageC(pc)
                            if pc["last"] and pc["hd"] + 2 < H:
                                load_head(pc["hd"] + 2)
                        if 0 <= step - 2 < n:
                            stageB(dbl[step - 2], dbl[step - 3] if step - 3 >= 0 else None)
                        if 0 <= step - 1 < n:
                            stageA2(dbl[step - 1])
                        if step < n:
                            stageA1(dbl[step])
                    out_proj_postnorm(l, qt, gs, src, B_dram_h[qt], dst, B_dram_h[qt])
                state["ring"] = 8
                cx_.barrier()

        for l in range(depth):
            src = xT if l == 0 else hbuf
            dst = yT if l == depth - 1 else hbuf
            if l % 2 == 0:
                conv_layer(l, src, dst)
            else:
                attn_layer(l, src, dst)
        cx_.barrier()
    return nc


def _tile_x(xb):
    a = xb.reshape(NT, T, KC, 128)
    return np.ascontiguousarray(a.transpose(0, 3, 2, 1))


def _untile_y(yt):
    return np.ascontiguousarray(yt.transpose(0, 3, 2, 1)).reshape(S, D)


def _tile_w_in(w):
    a = w.reshape(KC, 128, 4, 16, 128)
    a = a.transpose(3, 1, 0, 2, 4)
    return np.ascontiguousarray(a).reshape(16, 128, 8, 1024)


def _tile_w_out(w):
    a = w.reshape(KC, 128, 4, 512)
    a = a.transpose(2, 1, 0, 3)
    return np.ascontiguousarray(a).reshape(4, 128, 8, 1024)


def _gvec(g):
    return np.ascontiguousarray(g.reshape(KC, 128).T)


def _cwt(cwk):
    a = cwk.reshape(3, KC, 128).transpose(2, 1, 0)
    return np.ascontiguousarray(a).reshape(128, KC * 3)


def _consts():
    c = np.zeros((128, 3 * 128 + 4 * T), np.float32)
    c[:, 0:128] = 1.0
    j = np.arange(128)[:, None]
    k = np.arange(128)[None, :]
    c[:, 128:256] = np.where(j >= k, -1.0, 0.0)
    c[:, 256:384] = -1.0
    col = np.arange(T)[None, :]
    for i, d in enumerate((3, 2, 1, 0)):
        c[:, 384 + i * T: 384 + (i + 1) * T] = (col > 128 * d + j).astype(np.float32)
    return c


_NC_CACHE = {}


def kernel(**inputs):
    x = np.asarray(inputs["x"], dtype=np.float32)
    if "nc" not in _NC_CACHE:
        _NC_CACHE["nc"] = build_program()
    nc = _NC_CACHE["nc"]
    shared = {"cmat": _consts()}
    for l in range(DEPTH):
        if l % 2 == 0:
            shared[f"w_in{l}"] = _tile_w_in(np.asarray(inputs[f"conv_w_in_{l}"], np.float32))
            shared[f"w_out{l}"] = _tile_w_out(np.asarray(inputs[f"conv_w_out_{l}"], np.float32))
            shared[f"cw{l}"] = _cwt(np.asarray(inputs[f"conv_w_{l}"], np.float32))
        else:
            shared[f"w_in{l}"] = _tile_w_in(np.asarray(inputs[f"sb_w_in_{l}"], np.float32))
            shared[f"w_out{l}"] = _tile_w_out(np.asarray(inputs[f"sb_w_out_{l}"], np.float32))
        shared[f"g_pre{l}"] = _gvec(np.asarray(inputs[f"ln_pre_{l}"], np.float32))
        shared[f"g_post{l}"] = _gvec(np.asarray(inputs[f"ln_post_{l}"], np.float32))
    xts = [_tile_x(x[b]) for b in range(B)]
    zeros = {k: np.zeros_like(v) for k, v in shared.items() if k != "cmat"}
    zeros["cmat"] = shared["cmat"]
    zx = np.zeros_like(xts[0])
    in_maps = []
    for c in range(NCORES):
        if c % 2 == 0:
            m = dict(shared)
            m["xT"] = xts[c // 2]
        else:
            m = dict(zeros)
            m["xT"] = zx
        in_maps.append(m)
    res = run_bass_kernel_spmd(nc, in_maps, core_ids=list(range(NCORES)))
    out = np.stack([_untile_y(np.asarray(res.results[2 * b]["yT"])) for b in range(B)], axis=0)
    return out.astype(np.float32)
```

```python
import math
import numpy as np
import concourse.bass as bass
import concourse.mybir as mybir
from concourse.bass_utils import run_bass_kernel_spmd

F32 = mybir.dt.float32
BF16 = mybir.dt.bfloat16
AF = mybir.ActivationFunctionType
ALU = mybir.AluOpType

D = 2048
B = 4
S = 4096
DEPTH = 4
H = 16
DH = 128
T = 512
NT = S // T
KC = D // 128
EPS = 1e-6
NCORES = 8


class Sem:
    def __init__(self, nc, stack, name):
        self.h = stack.enter_context(nc.semaphore(name))
        self.name = name
        self.issued = 0


class Buf:
    def __init__(self, name):
        self.name = name
        self.w = {}
        self.r = {}


class Eng:
    def __init__(self, nc, stack, name, handle, is_queue=False):
        self.name = name
        self.h = handle
        self.sem = None if is_queue else Sem(nc, stack, "s_" + name)
        self.seen = {}

    def wait(self, sem, val):
        if val <= 0:
            return
        if self.seen.get(sem, 0) >= val:
            return
        self.h.wait_ge(sem.h, val)
        self.seen[sem] = val


class Ctx:
    def __init__(self, nc, stack):
        self.nc = nc
        self.stack = stack
        self.pe = Eng(nc, stack, "pe", nc.tensor)
        self.act = Eng(nc, stack, "act", nc.scalar)
        self.dve = Eng(nc, stack, "dve", nc.vector)
        self.sync = Eng(nc, stack, "sync", nc.sync, is_queue=True)
        self.pool = Eng(nc, stack, "pool", nc.gpsimd, is_queue=True)
        self.engs = [self.pe, self.act, self.dve, self.sync, self.pool]
        self.sems = [self.pe.sem, self.act.sem, self.dve.sem]
        self.nbuf = 0

    def buf(self, name):
        return Buf(name)

    def dsem(self, name):
        s = Sem(self.nc, self.stack, "d_" + name)
        self.sems.append(s)
        return s

    def _waits(self, eng, reads, writes):
        need = {}
        for b in reads:
            for s, v in b.w.items():
                need[s] = max(need.get(s, 0), v)
        for b in writes:
            for s, v in b.w.items():
                if s is eng.sem:
                    continue
                need[s] = max(need.get(s, 0), v)
            for s, v in b.r.items():
                if s is eng.sem:
                    continue
                need[s] = max(need.get(s, 0), v)
        for s, v in need.items():
            eng.wait(s, v)

    def op(self, eng, fn, reads=(), writes=(), inc=True):
        self._waits(eng, reads, writes)
        ins = fn()
        if inc:
            ins.then_inc(eng.sem.h, 1)
            eng.sem.issued += 1
            v = eng.sem.issued
            for b in writes:
                b.w[eng.sem] = v
            for b in reads:
                b.r[eng.sem] = v
        return ins

    def dma(self, q, sem, out, in_, reads=(), writes=(), **kw):
        self._waits(q, reads, writes)
        q.h.dma_start(out=out, in_=in_, **kw).then_inc(sem.h, 16)
        sem.issued += 16
        for b in writes:
            b.w[sem] = sem.issued
        for b in reads:
            b.r[sem] = sem.issued

    def barrier(self):
        for e in self.engs:
            for s in self.sems:
                e.wait(s, s.issued)


def build_program(depth=DEPTH, ntiles=NT):
    from contextlib import ExitStack
    nc = bass.Bass("TRN2", target_bir_lowering=False)
    dt = nc.dram_tensor
    xT = dt("xT", [NT, 128, KC, T], F32, kind="ExternalInput").ap()
    yT = dt("yT", [NT, 128, KC, T], F32, kind="ExternalOutput").ap()
    w_in, w_out, g_pre, g_post, cw = [], [], [], [], []
    for l in range(depth):
        w_in.append(dt(f"w_in{l}", [16, 128, 8, 1024], F32, kind="ExternalInput").ap())
        w_out.append(dt(f"w_out{l}", [4, 128, 8, 1024], F32, kind="ExternalInput").ap())
        g_pre.append(dt(f"g_pre{l}", [128, KC], F32, kind="ExternalInput").ap())
        g_post.append(dt(f"g_post{l}", [128, KC], F32, kind="ExternalInput").ap())
        if l % 2 == 0:
            cw.append(dt(f"cw{l}", [128, KC * 3], F32, kind="ExternalInput").ap())
        else:
            cw.append(None)
    cmat = dt("cmat", [128, 3 * 128 + 4 * T], F32, kind="ExternalInput").ap()
    hbuf = dt("hbuf", [NT, 128, KC, T], F32, kind="Internal").ap()
    QT = dt("QT", [H, 128, S], BF16, kind="Internal").ap()
    KT = dt("KT", [H, 128, S], BF16, kind="Internal").ap()
    ZT = dt("ZT", [H, 128, S], BF16, kind="Internal").ap()
    VV = dt("VV", [H, 128, S // 128, DH], BF16, kind="Internal").ap()

    with ExitStack() as st:
        cx_ = Ctx(nc, st)
        pe, act, dve, sync, pool = cx_.pe, cx_.act, cx_.dve, cx_.sync, cx_.pool
        sb = lambda name, shape, dtype: st.enter_context(nc.sbuf_tensor(name, shape, dtype))

        hst = sb("hst", [128, KC, T], F32)
        gated = sb("gated", [128, 2, KC, T], BF16)
        wbuf = sb("wbuf", [128, 2, 8, 1024], BF16)
        mbuf = sb("mbuf", [128, KC, T], F32)
        cst = sb("cst", [128, 3 * 128 + 4 * T], BF16)
        gpre_sb = sb("gpre_sb", [128, depth, KC], F32)
        gpost_sb = sb("gpost_sb", [128, depth, KC], F32)
        cw_sb = sb("cw_sb", [128, depth, KC * 3], F32)
        rstd = sb("rstd", [128, 2, T], F32)
        lnt = sb("lnt", [128, T], F32)
        sqr = sb("sqr", [128, 2, T], BF16)
        tmpf = sb("tmpf", [128, 2, T], F32)
        ps = st.enter_context(nc.psum_tensor("ps", [128, 8, T], F32))

        B_hst = Buf("hst"); D_hst = cx_.dsem("hst")
        B_gated = [Buf("gated0"), Buf("gated1")]
        B_w = [Buf("w0"), Buf("w1")]; D_w = [cx_.dsem("w0"), cx_.dsem("w1")]
        B_m = Buf("m")
        B_cst = Buf("cst"); D_cst = cx_.dsem("cst")
        B_rstd = [Buf("rstd0"), Buf("rstd1")]
        B_lnt = Buf("lnt")
        B_sqr = [Buf("sqr0"), Buf("sqr1")]
        B_tmpf = [Buf("tmpf0"), Buf("tmpf1")]
        B_ps = [Buf(f"ps{i}") for i in range(8)]
        B_dram_h = [Buf(f"h{t}") for t in range(NT)]
        state = {"bank": 0, "w": 0, "sq": 0, "tf": 0, "ring": 8}

        def next_bank():
            b = state["bank"] % state["ring"]; state["bank"] = (b + 1) % state["ring"]
            return b

        ones = cst[:, 0:128]
        negtri = cst[:, 128:256]
        negones = cst[:, 256:384]
        masks = [cst[:, 384 + d * T: 384 + (d + 1) * T] for d in range(4)]

        CW = 3 * 128 + 4 * T
        for c0 in range(0, CW, 1216):
            cx_.dma(pool, D_cst, cst[:, c0:c0 + 1216], cmat[:, c0:c0 + 1216], writes=[B_cst])
        for l in range(depth):
            cx_.dma(sync, D_cst, gpre_sb[:, l, :], g_pre[l][:], writes=[B_cst])
            cx_.dma(sync, D_cst, gpost_sb[:, l, :], g_post[l][:], writes=[B_cst])
            if cw[l] is not None:
                cx_.dma(sync, D_cst, cw_sb[:, l, :], cw[l][:], writes=[B_cst])

        def load_w(src_tile):
            s = state["w"]; state["w"] = 1 - s
            cx_.dma(pool, D_w[s], wbuf[:, s], src_tile, writes=[B_w[s]])
            return s

        def mm_group(bank, pairs, reads, final_inc=True, out_ap=None):
            o = ps[:, bank, :] if out_ap is None else out_ap
            n = len(pairs)
            for k, (l_, r_) in enumerate(pairs):
                last = (k == n - 1)
                cx_.op(pe, lambda l_=l_, r_=r_, k=k, last=last: nc.tensor.matmul(
                    o, lhsT=l_, rhs=r_, start=(k == 0), stop=last),
                    reads=reads, writes=[B_ps[bank]], inc=(last and final_inc))

        def rstd_from_bank(bank, slot):
            cx_.op(act, lambda: nc.scalar.activation(out=lnt[:], in_=ps[:, bank, :], func=AF.Ln,
                                                     scale=1.0 / D, bias=EPS),
                   reads=[B_ps[bank]], writes=[B_lnt])
            cx_.op(act, lambda: nc.scalar.activation(out=rstd[:, slot, :], in_=lnt[:], func=AF.Exp,
                                                     scale=-0.5),
                   reads=[B_lnt], writes=[B_rstd[slot]])

        def build_uT(l, ti, src, B_src_tile, uT_ap, B_uT):
            cx_.dma(sync, D_hst, hst[:], src[ti], reads=[B_src_tile], writes=[B_hst])
            cx_.op(act, lambda: nc.scalar.activation(out=uT_ap, in_=hst[:], func=AF.Square),
                   reads=[B_hst], writes=[B_uT])
            bank = next_bank()
            mm_group(bank, [(ones, uT_ap[:, kc, :]) for kc in range(KC)], reads=[B_cst, B_uT])
            rstd_from_bank(bank, 0)
            for kc in range(KC):
                cx_.op(dve, lambda kc=kc: nc.vector.scalar_tensor_tensor(
                    out=uT_ap[:, kc, :], in0=hst[:, kc, :], scalar=gpre_sb[:, l, kc:kc + 1],
                    in1=rstd[:, 0, :], op0=ALU.mult, op1=ALU.mult),
                    reads=[B_hst, B_rstd[0], B_cst], writes=[B_uT])

        def out_proj_postnorm(l, ti, gslot, src, B_src_tile, dst, B_dst_tile):
            stat_bank = next_bank()
            for ig in range(4):
                ws = load_w(w_out[l][ig])
                for ii in range(4):
                    i = ig * 4 + ii
                    bank = next_bank()
                    if bank == stat_bank:
                        bank = next_bank()
                    mm_group(bank, [(wbuf[:, ws].rearrange("p a b -> p (a b)")[:, j * T + ii * 128: j * T + (ii + 1) * 128],
                                     gated[:, gslot, j, :]) for j in range(KC)],
                             reads=[B_w[ws], B_gated[gslot]])
                    cx_.op(act, lambda i=i, bank=bank: nc.scalar.activation(
                        out=mbuf[:, i, :], in_=ps[:, bank, :], func=AF.Copy),
                        reads=[B_ps[bank]], writes=[B_m])
                    q = state["sq"]; state["sq"] = 1 - q
                    cx_.op(act, lambda q=q, bank=bank: nc.scalar.activation(
                        out=sqr[:, q, :], in_=ps[:, bank, :], func=AF.Square),
                        reads=[B_ps[bank]], writes=[B_sqr[q]])
                    cx_.op(pe, lambda q=q, i=i: nc.tensor.matmul(
                        ps[:, stat_bank, :], lhsT=ones, rhs=sqr[:, q, :], start=(i == 0), stop=(i == KC - 1)),
                        reads=[B_cst, B_sqr[q]], writes=[B_ps[stat_bank]], inc=True)
            rstd_from_bank(stat_bank, 1)
            cx_.dma(sync, D_hst, hst[:], src[ti], reads=[B_src_tile], writes=[B_hst])
            for i in range(KC):
                f = state["tf"]; state["tf"] = 1 - f
                cx_.op(dve, lambda i=i, f=f: nc.vector.scalar_tensor_tensor(
                    out=tmpf[:, f, :], in0=mbuf[:, i, :], scalar=gpost_sb[:, l, i:i + 1],
                    in1=rstd[:, 1, :], op0=ALU.mult, op1=ALU.mult),
                    reads=[B_m, B_rstd[1], B_cst], writes=[B_tmpf[f]])
                cx_.op(dve, lambda i=i, f=f: nc.vector.tensor_tensor(
                    out=hst[:, i, :], in0=hst[:, i, :], in1=tmpf[:, f, :], op=ALU.add),
                    reads=[B_hst, B_tmpf[f]], writes=[B_hst])
            cx_.dma(sync, D_hst, dst[ti], hst[:], reads=[B_hst], writes=[B_dst_tile])

        wflat = lambda s: wbuf[:, s].rearrange("p a b -> p (a b)")

        def conv_layer(l, src, dst):
            with ExitStack() as ls:
                lsb = lambda name, shape, dtype: ls.enter_context(nc.sbuf_tensor(f"{name}_c{l}", shape, dtype))
                uT = lsb("uT", [128, 2, KC, T], BF16)
                c_sb = lsb("c_sb", [128, 2, T], F32)
                cxb = lsb("cxb", [128, 2, T + 2], F32)
                yb = lsb("yb", [128, 2, T], F32)
                szb = lsb("szb", [128, 2, T], F32)
                tail = lsb("tail", [128, KC, 2], F32)
                B_uT = [Buf("uT0"), Buf("uT1")]
                B_c = [Buf("c0"), Buf("c1")]
                B_cx = [Buf("cx0"), Buf("cx1")]
                B_y = [Buf("y0"), Buf("y1")]
                B_sz = [Buf("sz0"), Buf("sz1")]
                B_tail = Buf("tail")
                cx_.op(dve, lambda: nc.vector.memset(tail[:], 0.0), writes=[B_tail])
                rot = 0
                for stile in range(ntiles // 2):
                    tiles = [2 * stile, 2 * stile + 1]
                    for s in range(2):
                        build_uT(l, tiles[s], src, B_dram_h[tiles[s]], uT[:, s], B_uT[s])
                    for j in range(KC):
                        ws = load_w(w_in[l][j])
                        wf = wflat(ws)
                        for s in range(2):
                            banks = []
                            for blk in range(4):
                                bank = next_bank()
                                banks.append(bank)
                                mm_group(bank, [(wf[:, kc * T + blk * 128: kc * T + (blk + 1) * 128], uT[:, s, kc, :])
                                                for kc in range(KC)], reads=[B_w[ws], B_uT[s]])
                            bB, bC, bX, bZ = banks
                            r = rot; rot = 1 - rot
                            cx_.op(act, lambda r=r, bC=bC: nc.scalar.activation(out=c_sb[:, r, :], in_=ps[:, bC, :], func=AF.Copy),
                                   reads=[B_ps[bC]], writes=[B_c[r]])
                            cx_.op(act, lambda r=r, bZ=bZ: nc.scalar.activation(out=szb[:, r, :], in_=ps[:, bZ, :], func=AF.Silu),
                                   reads=[B_ps[bZ]], writes=[B_sz[r]])
                            cx_.op(dve, lambda r=r, j=j: nc.vector.tensor_copy(out=cxb[:, r, 0:2], in_=tail[:, j, :]),
                                   reads=[B_tail], writes=[B_cx[r]])
                            cx_.op(dve, lambda r=r, bX=bX: nc.vector.tensor_tensor(out=cxb[:, r, 2:T + 2], in0=c_sb[:, r, :], in1=ps[:, bX, :], op=ALU.mult),
                                   reads=[B_c[r], B_ps[bX]], writes=[B_cx[r]])
                            cx_.op(dve, lambda r=r, j=j: nc.vector.tensor_copy(out=tail[:, j, :], in_=cxb[:, r, T:T + 2]),
                                   reads=[B_cx[r]], writes=[B_tail])
                            cwl = cw_sb[:, l, :]
                            cx_.op(dve, lambda r=r, j=j, cwl=cwl: nc.vector.tensor_scalar(
                                out=yb[:, r, :], in0=cxb[:, r, 0:T], scalar1=cwl[:, j * 3 + 0: j * 3 + 1], scalar2=None, op0=ALU.mult),
                                reads=[B_cx[r], B_cst], writes=[B_y[r]])
                            for k in (1, 2):
                                cx_.op(dve, lambda r=r, j=j, k=k, cwl=cwl: nc.vector.scalar_tensor_tensor(
                                    out=yb[:, r, :], in0=cxb[:, r, k:T + k], scalar=cwl[:, j * 3 + k: j * 3 + k + 1],
                                    in1=yb[:, r, :], op0=ALU.mult, op1=ALU.add),
                                    reads=[B_cx[r], B_y[r], B_cst], writes=[B_y[r]])
                            cx_.op(dve, lambda r=r, bB=bB: nc.vector.tensor_tensor(out=yb[:, r, :], in0=yb[:, r, :], in1=ps[:, bB, :], op=ALU.mult),
                                   reads=[B_y[r], B_ps[bB]], writes=[B_y[r]])
                            cx_.op(dve, lambda r=r, j=j, s=s: nc.vector.tensor_tensor(out=gated[:, s, j, :], in0=yb[:, r, :], in1=szb[:, r, :], op=ALU.mult),
                                   reads=[B_y[r], B_sz[r]], writes=[B_gated[s]])
                    for s in range(2):
                        out_proj_postnorm(l, tiles[s], s, src, B_dram_h[tiles[s]], dst, B_dram_h[tiles[s]])
                cx_.barrier()

        def attn_layer(l, src, dst):
            B_scr = Buf("qkvz_scratch")
            with ExitStack() as ls:
                lsb = lambda name, shape, dtype: ls.enter_context(nc.sbuf_tensor(f"{name}_a{l}", shape, dtype))
                uT = lsb("uTa", [128, 2, KC, T], BF16)
                qo = lsb("qo", [128, 2, T], BF16)
                ko = lsb("ko", [128, 2, T], BF16)
                zo = lsb("zo", [128, 2, T], BF16)
                vo = lsb("vo", [128, 2, 4, DH], BF16)
                B_uT = [Buf("uTa0"), Buf("uTa1")]
                B_o = {n: [Buf(n + "0"), Buf(n + "1")] for n in "qkzv"}
                D_o = {n: [cx_.dsem(f"o{n}{l}_0"), cx_.dsem(f"o{n}{l}_1")] for n in "qkzv"}
                rot = 0
                scale = 1.0 / math.sqrt(DH)
                for stile in range(ntiles // 2):
                    tiles = [2 * stile, 2 * stile + 1]
                    for s in range(2):
                        build_uT(l, tiles[s], src, B_dram_h[tiles[s]], uT[:, s], B_uT[s])
                    for hd in range(H):
                        ws = load_w(w_in[l][hd])
                        wf = wflat(ws)
                        for s in range(2):
                            ti = tiles[s]
                            r = rot; rot = 1 - rot
                            tok = slice(ti * T, (ti + 1) * T)
                            bq = next_bank()
                            mm_group(bq, [(wf[:, kc * T + 0: kc * T + 128], uT[:, s, kc, :]) for kc in range(KC)], reads=[B_w[ws], B_uT[s]])
                            bk = next_bank()
                            mm_group(bk, [(wf[:, kc * T + 128: kc * T + 256], uT[:, s, kc, :]) for kc in range(KC)], reads=[B_w[ws], B_uT[s]])
                            bz = next_bank()
                            mm_group(bz, [(wf[:, kc * T + 384: kc * T + 512], uT[:, s, kc, :]) for kc in range(KC)], reads=[B_w[ws], B_uT[s]])
                            bv = next_bank()
                            for blk in range(4):
                                mm_group(bv, [(uT[:, s, kc, blk * 128:(blk + 1) * 128], wf[:, kc * T + 256: kc * T + 384]) for kc in range(KC)],
                                         reads=[B_w[ws], B_uT[s]], out_ap=ps[:, bv, blk * 128:(blk + 1) * 128])
                            cx_.op(act, lambda r=r, bq=bq: nc.scalar.activation(out=qo[:, r, :], in_=ps[:, bq, :], func=AF.Copy, scale=scale),
                                   reads=[B_ps[bq]], writes=[B_o["q"][r]])
                            cx_.op(act, lambda r=r, bz=bz: nc.scalar.activation(out=zo[:, r, :], in_=ps[:, bz, :], func=AF.Silu),
                                   reads=[B_ps[bz]], writes=[B_o["z"][r]])
                            cx_.op(dve, lambda r=r, bk=bk: nc.vector.tensor_copy(out=ko[:, r, :], in_=ps[:, bk, :]),
                                   reads=[B_ps[bk]], writes=[B_o["k"][r]])
                            cx_.op(dve, lambda r=r, bv=bv: nc.vector.tensor_copy(out=vo[:, r].rearrange("p a b -> p (a b)"), in_=ps[:, bv, :]),
                                   reads=[B_ps[bv]], writes=[B_o["v"][r]])
                            cx_.dma(sync, D_o["q"][r], QT[hd, :, tok], qo[:, r, :], reads=[B_o["q"][r]], writes=[B_scr])
                            cx_.dma(sync, D_o["k"][r], KT[hd, :, tok], ko[:, r, :], reads=[B_o["k"][r]], writes=[B_scr])
                            cx_.dma(sync, D_o["z"][r], ZT[hd, :, tok], zo[:, r, :], reads=[B_o["z"][r]], writes=[B_scr])
                            cx_.dma(sync, D_o["v"][r], VV[hd, :, ti * 4:(ti + 1) * 4, :], vo[:, r], reads=[B_o["v"][r]], writes=[B_scr])
                cx_.barrier()
            with ExitStack() as ls:
                lsb = lambda name, shape, dtype: ls.enter_context(nc.sbuf_tensor(f"{name}_b{l}", shape, dtype))
                ktb = lsb("ktb", [128, 2, S], BF16)
                vvb = lsb("vvb", [128, 2, S // 128, DH], BF16)
                qtb = lsb("qtb", [128, 2, T], BF16)
                ztb = lsb("ztb", [128, 2, T], BF16)
                NE, NSP, NW_ = 2, 3, 2
                eb = lsb("eb", [128, NE, 2, T], F32)
                spb = lsb("spb", [128, NSP, 2, T], BF16)
                Sb = lsb("Sb", [128, 3, T], BF16)
                tsb = lsb("tsb", [128, 2, T], BF16)
                Wb = lsb("Wb", [128, NW_, 2, T], BF16)
                B_kv = [Buf("kv0"), Buf("kv1")]; D_kv = [cx_.dsem(f"kv{l}_0"), cx_.dsem(f"kv{l}_1")]
                B_e = [Buf(f"e{i}") for i in range(NE)]
                B_sp = [Buf(f"sp{i}") for i in range(NSP)]
                B_S = [Buf(f"S{i}") for i in range(3)]
                B_ts = [Buf("ts0"), Buf("ts1")]
                B_W = [Buf(f"W{i}") for i in range(NW_)]
                cnt = {"e": 0, "sp": 0, "S": 0, "W": 0, "kv": 0, "pair": 0, "ts": 0}
                state["ring"] = 6

                def rotn(name, n):
                    v = cnt[name]; cnt[name] = (v + 1) % n
                    return v

                def next_pair():
                    g = rotn("pair", 3)
                    return 2 * g

                for qt in range(ntiles):
                    gs = qt % 2
                    nkb = (qt + 1) * 4
                    dbl = []
                    for hd in range(H):
                        for idx in range(nkb // 2):
                            khi = nkb - 1 - 2 * idx
                            dbl.append(dict(hd=hd, khi=khi, klo=khi - 1, first=(idx == 0), last=(khi - 1 == 0),
                                            diag=((0 if khi - qt * 4 == 3 else 1) if khi >= qt * 4 else None)))
                    head_slot = {}

                    def load_head(hd):
                        sl = rotn("kv", 2)
                        head_slot[hd] = sl
                        nk = nkb * 128
                        cx_.dma(sync, D_kv[sl], ktb[:, sl, 0:nk], KT[hd, :, 0:nk], reads=[B_scr], writes=[B_kv[sl]])
                        cx_.dma(sync, D_kv[sl], vvb[:, sl, 0:nkb, :], VV[hd, :, 0:nkb, :], reads=[B_scr], writes=[B_kv[sl]])
                        cx_.dma(sync, D_kv[sl], qtb[:, sl, :], QT[hd, :, qt * T:(qt + 1) * T], reads=[B_scr], writes=[B_kv[sl]])
                        cx_.dma(sync, D_kv[sl], ztb[:, sl, :], ZT[hd, :, qt * T:(qt + 1) * T], reads=[B_scr], writes=[B_kv[sl]])

                    def kslice(p, kb):
                        return ktb[:, p["sl"], kb * 128:(kb + 1) * 128]

                    def stageA1_pe(p):
                        sl = head_slot[p["hd"]]
                        p["sl"] = sl
                        b0 = next_pair()
                        p["bL"] = b0
                        mm_group(b0, [(kslice(p, p["khi"]), qtb[:, sl, :])], reads=[B_kv[sl]])
                        mm_group(b0 + 1, [(kslice(p, p["klo"]), qtb[:, sl, :])], reads=[B_kv[sl]])

                    def stageA1_act(p):
                        b0 = p["bL"]
                        ei = rotn("e", NE)
                        cx_.op(act, lambda: nc.scalar.activation(out=eb[:, ei], in_=ps[:, b0:b0 + 2, :], func=AF.Exp),
                               reads=[B_ps[b0], B_ps[b0 + 1]], writes=[B_e[ei]])
                        p["e"] = ei

                    def stageA2(p):
                        ei = p["e"]
                        si = rotn("sp", NSP)
                        cx_.op(act, lambda: nc.scalar.activation(out=spb[:, si], in_=eb[:, ei], func=AF.Ln, bias=1.0),
                               reads=[B_e[ei]], writes=[B_sp[si]])
                        if p["diag"] is not None:
                            mk = cst[:, 384 + p["diag"] * 2 * T: 384 + (p["diag"] + 1) * 2 * T]
                            cx_.op(dve, lambda: nc.vector.tensor_tensor(out=spb[:, si].rearrange("p a b -> p (a b)"),
                                                                        in0=spb[:, si].rearrange("p a b -> p (a b)"), in1=mk, op=ALU.mult),
                                   reads=[B_sp[si], B_cst], writes=[B_sp[si]])
                        p["sp"] = si

                    def stageB_pe(p, prev):
                        sl, si = p["sl"], p["sp"]
                        carry = None if p["first"] else prev["S_next"]
                        b0 = next_pair()
                        mms = [(kslice(p, p["khi"]), qtb[:, sl, :]), (negtri, spb[:, si, 0, :])]
                        rd = [B_kv[sl], B_cst, B_sp[si]]
                        if carry is not None:
                            mms.append((negones, carry[0])); rd.append(carry[1])
                        mm_group(b0, mms, reads=rd)
                        mms = [(kslice(p, p["klo"]), qtb[:, sl, :]), (negtri, spb[:, si, 1, :]), (negones, spb[:, si, 0, :])]
                        if carry is not None:
                            mms.append((negones, carry[0]))
                        mm_group(b0 + 1, mms, reads=rd)
                        p["bX"] = b0
                        p["carry"] = carry

                    def stageB_act(p):
                        sl, si = p["sl"], p["sp"]
                        b0 = p["bX"]
                        carry = p["carry"]
                        wi = rotn("W", NW_)
                        cx_.op(act, lambda: nc.scalar.activation(out=Wb[:, wi], in_=ps[:, b0:b0 + 2, :], func=AF.Exp),
                               reads=[B_ps[b0], B_ps[b0 + 1]], writes=[B_W[wi]])
                        if p["diag"] is not None:
                            mk = cst[:, 384 + p["diag"] * 2 * T: 384 + (p["diag"] + 1) * 2 * T]
                            cx_.op(dve, lambda: nc.vector.tensor_tensor(out=Wb[:, wi].rearrange("p a b -> p (a b)"),
                                                                        in0=Wb[:, wi].rearrange("p a b -> p (a b)"), in1=mk, op=ALU.mult),
                                   reads=[B_W[wi], B_cst], writes=[B_W[wi]])
                        p["W"] = wi
                        if not p["last"]:
                            sn = rotn("S", 3)
                            if carry is None:
                                cx_.op(dve, lambda: nc.vector.tensor_tensor(out=Sb[:, sn, :], in0=spb[:, si, 0, :], in1=spb[:, si, 1, :], op=ALU.add),
                                       reads=[B_sp[si]], writes=[B_S[sn]])
                            else:
                                ti_ = rotn("ts", 2)
                                cx_.op(dve, lambda: nc.vector.tensor_tensor(out=tsb[:, ti_, :], in0=spb[:, si, 0, :], in1=spb[:, si, 1, :], op=ALU.add),
                                       reads=[B_sp[si]], writes=[B_ts[ti_]])
                                cx_.op(dve, lambda: nc.vector.tensor_tensor(out=Sb[:, sn, :], in0=tsb[:, ti_, :], in1=carry[0], op=ALU.add),
                                       reads=[B_ts[ti_], carry[1]], writes=[B_S[sn]])
                            p["S_next"] = (Sb[:, sn, :], B_S[sn])

                    def stageC(p):
                        hd, sl, wi = p["hd"], p["sl"], p["W"]
                        bO = 6 + (hd % 2)
                        cx_.op(pe, lambda: nc.tensor.matmul(ps[:, bO, :], lhsT=vvb[:, sl, p["khi"], :], rhs=Wb[:, wi, 0, :],
                                                            start=p["first"], stop=False),
                               reads=[B_kv[sl], B_W[wi]], writes=[B_ps[bO]], inc=False)
                        cx_.op(pe, lambda: nc.tensor.matmul(ps[:, bO, :], lhsT=vvb[:, sl, p["klo"], :], rhs=Wb[:, wi, 1, :],
                                                            start=False, stop=p["last"]),
                               reads=[B_kv[sl], B_W[wi]], writes=[B_ps[bO]], inc=True)
                        if p["last"]:
                            cx_.op(dve, lambda: nc.vector.tensor_tensor(out=gated[:, gs, hd, :], in0=ztb[:, sl, :], in1=ps[:, bO, :], op=ALU.mult),
                                   reads=[B_kv[sl], B_ps[bO]], writes=[B_gated[gs]])

                    n = len(dbl)
                    load_head(0)
                    load_head(1)
                    c_first = (nkb // 2) < 3
                    for step in range(n + 3):
                        pC = dbl[step - 3] if 0 <= step - 3 < n else None
                        pB = dbl[step - 2] if 0 <= step - 2 < n else None
                        pA2 = dbl[step - 1] if 0 <= step - 1 < n else None
                        pA1 = dbl[step] if step < n else None

                        def doC():
                            if pC is not None:
                                stageC(pC)
                                if pC["last"] and pC["hd"] + 2 < H:
                                    load_head(pC["hd"] + 2)
                        if c_first:
                            doC()
                        if pA1 is not None:
                            stageA1_pe(pA1)
                        if pA2 is not None:
                            stageA2(pA2)
                        if pB is not None:
                            stageB_pe(pB, dbl[step - 3] if step - 3 >= 0 else None)
                        if not c_first:
                            doC()
                        if pA1 is not None:
                            stageA1_act(pA1)
                        if pB is not None:
                            stageB_act(pB)
                    out_proj_postnorm(l, qt, gs, src, B_dram_h[qt], dst, B_dram_h[qt])
                state["ring"] = 8
                cx_.barrier()

        for l in range(depth):
            src = xT if l == 0 else hbuf
            dst = yT if l == depth - 1 else hbuf
            if l % 2 == 0:
                conv_layer(l, src, dst)
            else:
                attn_layer(l, src, dst)
        cx_.barrier()
    return nc


def _tile_x(xb):
    a = xb.reshape(NT, T, KC, 128)
    return np.ascontiguousarray(a.transpose(0, 3, 2, 1))


def _untile_y(yt):
    return np.ascontiguousarray(yt.transpose(0, 3, 2, 1)).reshape(S, D)


def _tile_w_in(w):
    a = w.reshape(KC, 128, 4, 16, 128)
    a = a.transpose(3, 1, 0, 2, 4)
    return np.ascontiguousarray(a).reshape(16, 128, 8, 1024)


def _tile_w_out(w):
    a = w.reshape(KC, 128, 4, 512)
    a = a.transpose(2, 1, 0, 3)
    return np.ascontiguousarray(a).reshape(4, 128, 8, 1024)


def _gvec(g):
    return np.ascontiguousarray(g.reshape(KC, 128).T)


def _cwt(cwk):
    a = cwk.reshape(3, KC, 128).transpose(2, 1, 0)
    return np.ascontiguousarray(a).reshape(128, KC * 3)


def _consts():
    c = np.zeros((128, 3 * 128 + 4 * T), np.float32)
    c[:, 0:128] = 1.0
    j = np.arange(128)[:, None]
    k = np.arange(128)[None, :]
    c[:, 128:256] = np.where(j >= k, -1.0, 0.0)
    c[:, 256:384] = -1.0
    col = np.arange(T)[None, :]
    for i, d in enumerate((3, 2, 1, 0)):
        c[:, 384 + i * T: 384 + (i + 1) * T] = (col > 128 * d + j).astype(np.float32)
    return c


_NC_CACHE = {}


def kernel(**inputs):
    x = np.asarray(inputs["x"], dtype=np.float32)
    if "nc" not in _NC_CACHE:
        _NC_CACHE["nc"] = build_program()
    nc = _NC_CACHE["nc"]
    shared = {"cmat": _consts()}
    for l in range(DEPTH):
        if l % 2 == 0:
            shared[f"w_in{l}"] = _tile_w_in(np.asarray(inputs[f"conv_w_in_{l}"], np.float32))
            shared[f"w_out{l}"] = _tile_w_out(np.asarray(inputs[f"conv_w_out_{l}"], np.float32))
            shared[f"cw{l}"] = _cwt(np.asarray(inputs[f"conv_w_{l}"], np.float32))
        else:
            shared[f"w_in{l}"] = _tile_w_in(np.asarray(inputs[f"sb_w_in_{l}"], np.float32))
            shared[f"w_out{l}"] = _tile_w_out(np.asarray(inputs[f"sb_w_out_{l}"], np.float32))
        shared[f"g_pre{l}"] = _gvec(np.asarray(inputs[f"ln_pre_{l}"], np.float32))
        shared[f"g_post{l}"] = _gvec(np.asarray(inputs[f"ln_post_{l}"], np.float32))
    xts = [_tile_x(x[b]) for b in range(B)]
    zeros = {k: np.zeros_like(v) for k, v in shared.items() if k != "cmat"}
    zeros["cmat"] = shared["cmat"]
    zx = np.zeros_like(xts[0])
    in_maps = []
    for c in range(NCORES):
        if c % 2 == 0:
            m = dict(shared)
            m["xT"] = xts[c // 2]
        else:
            m = dict(zeros)
            m["xT"] = zx
        in_maps.append(m)
    res = run_bass_kernel_spmd(nc, in_maps, core_ids=list(range(NCORES)))
    out = np.stack([_untile_y(np.asarray(res.results[2 * b]["yT"])) for b in range(B)], axis=0)
    return out.astype(np.float32)
```
